# Optimizing a Trainium2 kernel written in Bass

```python
import jax
import jax.numpy as jnp
from jax import lax
import numpy as np


D_MODEL = 1024
BATCH = 2
SEQ = 16384
DEPTH = 2

CTX_LEN = 256
GRID_W = 64
GLA_HEADS = 4
GLA_KEY = D_MODEL // 2
GLA_VAL = D_MODEL
GLA_DK = GLA_KEY // GLA_HEADS
GLA_DV = GLA_VAL // GLA_HEADS
GLA_RANK = 16
GLA_CHUNK = 64
GLA_GATE_NORM = 16.0
CONV_D = D_MODEL
CONV_K = 31
SGU_D = D_MODEL
SGU_GROUPS = 8
SGU_GC = SGU_D // SGU_GROUPS
SGU_CHUNK = 128
N_BRANCH = 3
D_FF = ((8 * D_MODEL + 3 * 256 - 1) // (3 * 256)) * 256
IN_SPLITS = (GLA_KEY, GLA_KEY, GLA_VAL, 2 * GLA_RANK, GLA_VAL, 2 * CONV_D, 2 * SGU_D, N_BRANCH * D_MODEL)
IN_COLS = sum(IN_SPLITS)
EPS = 1e-6

kernel_name = 'hybrid_gla_conv_sgu_dit_block'


def rms_norm(x, g):
    xf = x.astype(jnp.float32)
    y = xf * lax.rsqrt(jnp.mean(xf * xf, axis=-1, keepdims=True) + EPS)
    return (y * g.astype(jnp.float32)).astype(x.dtype)


def layer_norm(x, g, b):
    xf = x.astype(jnp.float32)
    xc = xf - jnp.mean(xf, axis=-1, keepdims=True)
    y = xc * lax.rsqrt(jnp.mean(xc * xc, axis=-1, keepdims=True) + EPS)
    return (y * g.astype(jnp.float32) + b.astype(jnp.float32)).astype(x.dtype)


def modulate(h, shift, scale):
    return h * (1.0 + scale) + shift


def flip_seq(t):
    return jnp.flip(t, axis=1)


def gla_log_gates(a_dn, a_up, a_b):
    bsz, L = a_dn.shape[:2]
    z = jnp.einsum('bler,erk->blek', a_dn.reshape(bsz, L, 2, GLA_RANK), a_up) + a_b
    lg = jax.nn.log_sigmoid(z.astype(jnp.float32)) / GLA_GATE_NORM
    lg = lg.reshape(bsz, L, 2, GLA_HEADS, GLA_DK)
    return lg[:, :, 0], lg[:, :, 1]


def gla_chunked(q, k, v, logg, s0):
    bsz, L, H, DK = q.shape
    DV = v.shape[-1]
    n = L // GLA_CHUNK
    q = q.reshape(bsz, n, GLA_CHUNK, H, DK)
    k = k.reshape(bsz, n, GLA_CHUNK, H, DK)
    logg = logg.reshape(bsz, n, GLA_CHUNK, H, DK)
    v = v.reshape(bsz, n, GLA_CHUNK, H, DV)
    b = jnp.cumsum(logg, axis=2)
    b_last = b[:, :, -1:]
    q_e = q * jnp.exp(b)
    a = jnp.einsum('bnihd,bnjhd->bnhij', q_e, k * jnp.exp(-b))
    a = jnp.where(jnp.tril(jnp.ones((GLA_CHUNK, GLA_CHUNK), dtype=bool)), a, 0.0)
    o_intra = jnp.einsum('bnhij,bnjhv->bnihv', a, v)
    k_dec = k * jnp.exp(b_last - b)

    def step(s, xs):
        q_c, k_c, v_c, dec_c = xs
        o_c = jnp.einsum('bchd,bhdv->bchv', q_c, s)
        s = dec_c[..., None] * s + jnp.einsum('bchd,bchv->bhdv', k_c, v_c)
        return s, o_c

    xs = (jnp.moveaxis(q_e, 1, 0), jnp.moveaxis(k_dec, 1, 0), jnp.moveaxis(v, 1, 0),
          jnp.moveaxis(jnp.exp(b_last[:, :, 0]), 1, 0))
    s_fin, o_inter = lax.scan(step, s0, xs)
    o = o_intra + jnp.moveaxis(o_inter, 0, 1)
    return o.reshape(bsz, L, H, DV), s_fin


def gla_bidir(q, k, v, lg_f, lg_b, s0_f, s0_b):
    o_f, s_f = gla_chunked(q, k, v, lg_f, s0_f)
    o_b, s_b = gla_chunked(flip_seq(q), flip_seq(k), flip_seq(v), flip_seq(lg_b), s0_b)
    diag = jnp.sum(q * k, axis=-1, keepdims=True) * v
    return o_f + flip_seq(o_b) - diag, s_f, s_b


def gla_ctx_state(k, v, logg):
    b = jnp.cumsum(logg, axis=1)
    return jnp.einsum('blhd,blhv->bhdv', k * jnp.exp(b[:, -1:] - b), v)


def depthwise_conv(h, w, bias):
    pad = CONV_K // 2
    y = lax.conv_general_dilated(h, w.astype(h.dtype)[:, None, :], window_strides=(1,),
                                 padding=[(pad, pad)], dimension_numbers=('NWC', 'WIO', 'NWC'),
                                 feature_group_count=h.shape[-1])
    return y + bias


def swiglu(h, w1, w2):
    g, u = jnp.split(h @ w1, 2, axis=-1)
    return (jax.nn.silu(g) * u) @ w2


def mixer(p, lp, s0_f, s0_b, rows):
    bsz, L = p.shape[:2]
    dt = p.dtype
    f32 = jnp.float32
    q, k, v, a_dn, r, conv_in, sgu_in, gate_logit = jnp.split(
        p, np.cumsum(IN_SPLITS)[:-1].tolist(), axis=-1)

    qh = q.astype(f32).reshape(bsz, L, GLA_HEADS, GLA_DK) * (GLA_DK ** -0.5)
    kh = k.astype(f32).reshape(bsz, L, GLA_HEADS, GLA_DK)
    vh = v.astype(f32).reshape(bsz, L, GLA_HEADS, GLA_DV)
    lg_f, lg_b = gla_log_gates(a_dn, lp['gla_a_up'], lp['gla_a_b'])
    o, s_f, s_b = gla_bidir(qh, kh, vh, lg_f, lg_b, s0_f, s0_b)
    o = o * lax.rsqrt(jnp.mean(o * o, axis=-1, keepdims=True) + EPS)
    o = o.reshape(bsz, L, GLA_VAL) * lp['gla_norm_g'].astype(f32) * jax.nn.silu(r.astype(f32))
    y_a = o.astype(dt) @ lp['w_o_gla']

    c1, c2 = jnp.split(conv_in, 2, axis=-1)
    hc = c1 * jax.nn.sigmoid(c2)
    if rows is None:
        hc = depthwise_conv(hc, lp['conv_w'], lp['conv_b'])
    else:
        hc = depthwise_conv(hc.reshape(bsz * rows, GRID_W, CONV_D), lp['conv_w'],
                            lp['conv_b']).reshape(bsz, L, CONV_D)
    y_b = jax.nn.silu(layer_norm(hc, lp['conv_ln_g'], lp['conv_ln_b'])) @ lp['w_o_conv']

    su, sv = jnp.split(jax.nn.gelu(sgu_in), 2, axis=-1)
    sv = layer_norm(sv, lp['sgu_ln_g'], lp['sgu_ln_b']).reshape(
        bsz, L // SGU_CHUNK, SGU_CHUNK, SGU_GROUPS, SGU_GC)
    sp = jnp.einsum('gij,bnjgc->bnigc', lp['sgu_ws'], sv) + lp['sgu_b'].T[:, :, None]
    y_c = (su * sp.reshape(bsz, L, SGU_D)) @ lp['w_o_sgu']

    gates = jax.nn.sigmoid(gate_logit.astype(f32)).astype(dt).reshape(bsz, L, N_BRANCH, D_MODEL)
    y = gates[:, :, 0] * y_a + gates[:, :, 1] * y_b + gates[:, :, 2] * y_c
    return y @ lp['w_out'], s_f, s_b


def setup_inputs(seed: int = 0) -> dict:
    key = jax.random.key(seed)
    ks = iter(jax.random.split(key, 32))

    def nrm(shape, scale):
        return jax.random.normal(next(ks), shape, jnp.float32) * scale

    def gain(shape):
        return 1.0 + nrm(shape, 0.02)

    return {
        'x': nrm((BATCH, SEQ, D_MODEL), 1.0),
        'c': nrm((BATCH, D_MODEL), 1.0),
        'ctx': nrm((BATCH, CTX_LEN, D_MODEL), 1.0),
        'c_ctx': nrm((D_MODEL,), 1.0),
        'w_ada': nrm((DEPTH, D_MODEL, 6 * D_MODEL), D_MODEL ** -0.5),
        'b_ada': nrm((DEPTH, 6 * D_MODEL), 0.02),
        'norm1_g': gain((DEPTH, D_MODEL)),
        'norm2_g': gain((DEPTH, D_MODEL)),
        'w_in': nrm((DEPTH, D_MODEL, IN_COLS), D_MODEL ** -0.5),
        'gla_a_up': nrm((DEPTH, 2, GLA_RANK, GLA_KEY), GLA_RANK ** -0.5),
        'gla_a_b': nrm((DEPTH, 2, GLA_KEY), 0.1),
        'gla_norm_g': gain((DEPTH, GLA_VAL)),
        'w_o_gla': nrm((DEPTH, GLA_VAL, D_MODEL), GLA_VAL ** -0.5),
        'conv_w': nrm((DEPTH, CONV_K, CONV_D), CONV_K ** -0.5),
        'conv_b': nrm((DEPTH, CONV_D), 0.02),
        'conv_ln_g': gain((DEPTH, CONV_D)),
        'conv_ln_b': nrm((DEPTH, CONV_D), 0.02),
        'w_o_conv': nrm((DEPTH, CONV_D, D_MODEL), CONV_D ** -0.5),
        'sgu_ln_g': gain((DEPTH, SGU_D)),
        'sgu_ln_b': nrm((DEPTH, SGU_D), 0.02),
        'sgu_ws': nrm((DEPTH, SGU_GROUPS, SGU_CHUNK, SGU_CHUNK), SGU_CHUNK ** -0.5),
        'sgu_b': gain((DEPTH, SGU_GROUPS, SGU_CHUNK)),
        'w_o_sgu': nrm((DEPTH, SGU_D, D_MODEL), SGU_D ** -0.5),
        'w_out': nrm((DEPTH, D_MODEL, D_MODEL), D_MODEL ** -0.5),
        'w_ffn_in': nrm((DEPTH, D_MODEL, 2 * D_FF), D_MODEL ** -0.5),
        'w_ffn_out': nrm((DEPTH, D_FF, D_MODEL), D_FF ** -0.5),
        'final_g': gain((D_MODEL,)),
    }


def reference(x, c, ctx, c_ctx, w_ada, b_ada, norm1_g, norm2_g, w_in, gla_a_up, gla_a_b,
              gla_norm_g, w_o_gla, conv_w, conv_b, conv_ln_g, conv_ln_b, w_o_conv, sgu_ln_g,
              sgu_ln_b, sgu_ws, sgu_b, w_o_sgu, w_out, w_ffn_in, w_ffn_out, final_g):
    bsz = x.shape[0]
    rows = x.shape[1] // GRID_W
    h_lat = x
    h_ctx = ctx
    silu_c = jax.nn.silu(c)[:, None, :]
    silu_cc = jax.nn.silu(c_ctx)[None, None, :]
    zero_state = jnp.zeros((bsz, GLA_HEADS, GLA_DK, GLA_DV), jnp.float32)
    for l in range(DEPTH):
        lp = {'gla_a_up': gla_a_up[l], 'gla_a_b': gla_a_b[l], 'gla_norm_g': gla_norm_g[l],
              'w_o_gla': w_o_gla[l], 'conv_w': conv_w[l], 'conv_b': conv_b[l],
              'conv_ln_g': conv_ln_g[l], 'conv_ln_b': conv_ln_b[l], 'w_o_conv': w_o_conv[l],
              'sgu_ln_g': sgu_ln_g[l], 'sgu_ln_b': sgu_ln_b[l], 'sgu_ws': sgu_ws[l],
              'sgu_b': sgu_b[l], 'w_o_sgu': w_o_sgu[l], 'w_out': w_out[l]}
        sh1, sc1, g1, sh2, sc2, g2 = jnp.split(silu_c @ w_ada[l] + b_ada[l], 6, axis=-1)
        csh1, csc1, cg1, csh2, csc2, cg2 = jnp.split(silu_cc @ w_ada[l] + b_ada[l], 6, axis=-1)

        hc = modulate(rms_norm(h_ctx, norm1_g[l]), csh1, csc1)
        if l == DEPTH - 1:
            kva = hc @ w_in[l][:, GLA_KEY:2 * GLA_KEY + GLA_VAL + 2 * GLA_RANK]
            kc, vc, ac = jnp.split(kva, [GLA_KEY, GLA_KEY + GLA_VAL], axis=-1)
            kc = kc.astype(jnp.float32).reshape(bsz, -1, GLA_HEADS, GLA_DK)
            vc = vc.astype(jnp.float32).reshape(bsz, -1, GLA_HEADS, GLA_DV)
            lgc_f, lgc_b = gla_log_gates(ac, gla_a_up[l], gla_a_b[l])
            s_f = gla_ctx_state(kc, vc, lgc_f)
            s_b = gla_ctx_state(flip_seq(kc), flip_seq(vc), flip_seq(lgc_b))
        else:
            yc, s_f, s_b = mixer(hc @ w_in[l], lp, zero_state, zero_state, None)
            h_ctx = h_ctx + cg1 * yc
            h_ctx = h_ctx + cg2 * swiglu(modulate(rms_norm(h_ctx, norm2_g[l]), csh2, csc2),
                                         w_ffn_in[l], w_ffn_out[l])

        hl = modulate(rms_norm(h_lat, norm1_g[l]), sh1, sc1)
        yl, _, _ = mixer(hl @ w_in[l], lp, s_f, s_b, rows)
        h_lat = h_lat + g1 * yl
        h_lat = h_lat + g2 * swiglu(modulate(rms_norm(h_lat, norm2_g[l]), sh2, sc2),
                                    w_ffn_in[l], w_ffn_out[l])
    return rms_norm(h_lat, final_g)
```

```python
import numpy as np
import concourse.bass as bass
import concourse.mybir as mybir
from concourse.bass_utils import run_bass_kernel_spmd

F32 = mybir.dt.float32
BF16 = mybir.dt.bfloat16
AF = mybir.ActivationFunctionType
ALU = mybir.AluOpType
AX = mybir.AxisListType

D = 1024
NCH = 8
DFF = 2816
NBLK = 47
EPS = 1e-6
CTX = 256
TL = 512
NREC = 4104


class Buf:
    __slots__ = ("name", "w", "r")

    def __init__(self, name):
        self.name = name
        self.w = None
        self.r = {}


class V:
    __slots__ = ("buf", "ap")

    def __init__(self, buf, ap):
        self.buf = buf
        self.ap = ap

    def __getitem__(self, idx):
        return V(self.buf, self.ap[idx])

    def re(self, s, **kw):
        return V(self.buf, self.ap.rearrange(s, **kw))


class Pool:
    def __init__(self, name, vs):
        self.name = name
        self.free = list(vs)

    def get(self):
        if not self.free:
            raise RuntimeError("pool empty " + self.name)
        return self.free.pop(0)

    def put(self, *vs):
        for v in vs:
            self.free.append(v)


class TK:
    def __init__(self, nc):
        self.nc = nc
        self.E = {"pe": nc.tensor, "dve": nc.vector, "act": nc.scalar, "pool": nc.gpsimd, "sp": nc.sync}
        self.sem = {}
        self.cnt = {}
        self.seen = {e: {} for e in self.E}
        for e in ("pe", "dve", "act", "pool"):
            self.sem[e] = nc.alloc_semaphore("c_" + e)
            self.cnt[e] = 0
        self.nins = 0
        self.pending = []

    def _sem(self, key):
        if key not in self.sem:
            self.sem[key] = self.nc.alloc_semaphore("d_" + key)
            self.cnt[key] = 0
        return self.sem[key]

    def _deps(self, reads, writes):
        d = {}

        def add(k, val):
            if d.get(k, 0) < val:
                d[k] = val

        for v in reads:
            if v.buf.w:
                add(*v.buf.w)
        for v in writes:
            if v.buf.w:
                add(*v.buf.w)
            for k, val in v.buf.r.items():
                add(k, val)
        return d

    def _wait(self, e, deps):
        for key, val in deps.items():
            if e == "pe" and key == "pe":
                continue
            if self.seen[e].get(key, 0) < val:
                self.E[e].wait_ge(self.sem[key], val)
                self.seen[e][key] = val

    def op(self, e, fn, reads, writes, tbl=None):
        w = writes[0].ap
        n = 1
        for d in w.shape[1:]:
            n *= d
        if e == "pe":
            dur = max(n, 64) / 1.95 + 40
            if reads and reads[0].ap.dtype == F32:
                dur *= 4
        elif e == "dve":
            dur = 0.95 * n + 60
        elif e == "act":
            dur = 1.0 * n + 120
        else:
            dur = 2.0 * n + 100
        self.pending.append(("op", e, fn, list(reads), list(writes), dur, dur, tbl))

    def dma(self, q, out, in_, key, pace=()):
        nb = 1
        for d in out.ap.shape:
            nb *= d
        nb *= 2 if out.ap.dtype == BF16 else 4
        busy = 60.0
        if key.startswith("cv"):
            busy = 10000.0
        self.pending.append(("dma", q, (out, in_, key, list(pace)), [in_], [out], busy, 2000.0 + busy + nb / 150.0, None, list(pace)))

    def custom(self, e, fn, reads, writes, dur):
        self.pending.append(("custom", e, fn, list(reads), list(writes), dur, dur, None))

    def flush(self, window=700):
        P = self.pending
        self.pending = []
        n = len(P)
        if n == 0:
            return
        preds = [None] * n
        lastw, readers = {}, {}
        for i, rec in enumerate(P):
            ps = set()
            for v in rec[3]:
                b = id(v.buf)
                if b in lastw:
                    ps.add(lastw[b])
            for v in rec[4]:
                b = id(v.buf)
                if b in lastw:
                    ps.add(lastw[b])
                for r in readers.get(b, ()):
                    ps.add(r)
            if len(rec) > 8:
                for v in rec[8]:
                    b = id(v.buf)
                    if b in lastw:
                        ps.add(lastw[b])
            ps.discard(i)
            preds[i] = ps
            for v in rec[3]:
                readers.setdefault(id(v.buf), []).append(i)
            for v in rec[4]:
                b = id(v.buf)
                lastw[b] = i
                readers[b] = []
        succs = [[] for _ in range(n)]
        indeg = [0] * n
        for i in range(n):
            indeg[i] = len(preds[i])
            for p in preds[i]:
                succs[p].append(i)
        bl = [0.0] * n
        for i in range(n - 1, -1, -1):
            m = 0.0
            for sc in succs[i]:
                if bl[sc] > m:
                    m = bl[sc]
            bl[i] = P[i][6] + m
        engs = ["pe", "dve", "act", "pool", "sp"]
        etime = {e: 0.0 for e in engs}
        etbl = {e: None for e in engs}
        rdy = {e: [] for e in engs}
        rt = [0.0] * n
        fin = [0.0] * n
        done = [False] * n
        for i in range(n):
            if indeg[i] == 0:
                rdy[P[i][1]].append(i)
        lo = 0
        order = []
        nsched = 0
        while nsched < n:
            while lo < n and done[lo]:
                lo += 1
            lim = lo + window
            best = None
            for e in engs:
                lst = rdy[e]
                if not lst:
                    continue
                et = etime[e]
                tb = etbl[e]
                for i in lst:
                    if i >= lim:
                        continue
                    st = rt[i] if rt[i] > et else et
                    t = P[i][7]
                    if t is not None and tb is not None and t != tb:
                        st += 1300.0
                    key = (int((st + 0.02 * (i - lo)) / SCHED_QUANT), -bl[i], i)
                    if best is None or key < best[0]:
                        best = (key, i, e, st)
            if best is None:
                cand = [(min(l), e) for e, l in rdy.items() if l]
                i, e = min(cand)
                st = max(rt[i], etime[e])
            else:
                _, i, e, st = best
            rec = P[i]
            rdy[e].remove(i)
            etime[e] = st + rec[5]
            if rec[7] is not None:
                etbl[e] = rec[7]
            fin[i] = st + rec[6]
            done[i] = True
            nsched += 1
            order.append(i)
            for sc in succs[i]:
                lat = 0.0 if P[sc][1] == e and rec[0] == "op" else 150.0
                if fin[i] + lat > rt[sc]:
                    rt[sc] = fin[i] + lat
                indeg[sc] -= 1
                if indeg[sc] == 0:
                    rdy[P[sc][1]].append(sc)
        self.sched_est = max(etime.values())
        for i in order:
            rec = P[i]
            if rec[0] == "op":
                self._emit_op(rec[1], rec[2], rec[3], rec[4])
            elif rec[0] == "dma":
                self._emit_dma(rec[1], *rec[2])
            else:
                rec[2](self)

    def _emit_op(self, e, fn, reads, writes):
        self._wait(e, self._deps(reads, writes))
        ins = fn(self.E[e])
        self.cnt[e] += 1
        ins.then_inc(self.sem[e], 1)
        self.nins += 1
        c = self.cnt[e]
        for v in reads:
            v.buf.r[e] = c
        for v in writes:
            v.buf.w = (e, c)
            v.buf.r = {}
        return ins

    def _emit_dma(self, q, out, in_, key, pace=()):
        self._wait(q, self._deps([in_] + list(pace), [out]))
        sem = self._sem(key)
        if self.cnt[key] > 0 and self.seen[q].get(key, 0) < self.cnt[key]:
            self.E[q].wait_ge(sem, self.cnt[key])
            self.seen[q][key] = self.cnt[key]
        ins = self.E[q].dma_start(out=out.ap, in_=in_.ap)
        self.cnt[key] += 16
        ins.then_inc(sem, 16)
        self.nins += 1
        c = self.cnt[key]
        in_.buf.r[key] = c
        out.buf.w = (key, c)
        out.buf.r = {}

    def wait_all(self, e, bufs):
        d = {}
        for b in bufs:
            if b.w and d.get(b.w[0], 0) < b.w[1]:
                d[b.w[0]] = b.w[1]
        for key, val in d.items():
            self.E[e].wait_ge(self.sem[key], val)


Q0, K0, V0, A0, R0, C10, C20, SU0, SV0, GT0 = 0, 512, 1024, 2048, 2080, 3104, 4128, 5152, 6176, 7200
B_QKV, B_R, B_CV, B_SU, B_SV, B_GA, B_GB, B_GC = 0, 4, 6, 10, 12, 14, 16, 18
B_OGLA, B_OCONV, B_OSGU, B_OUT, B_FIN, B_FOUT = 20, 22, 24, 26, 28, 39
P_N1G, P_N2G, P_GNG, P_CVB, P_CLG, P_CLB, P_BADA, P_CW, P_AB, P_FG, NPV = 0, 8, 16, 24, 32, 40, 48, 96, 344, 352, 360


P2_ORDER = ([0, 1, 4, 2, 3, 5, 14, 20, 15, 21] + [6, 47, 48, 7, 49, 50, 8, 51, 52, 9, 53, 54, 16, 22, 17, 23] + [12, 13, 10, 11, 18, 24, 19, 25]
            + [26, 27] + list(range(28, 39)) + list(range(39, 47)))
assert sorted(P2_ORDER) == list(range(NBLK + 8))


class Builder:
    def __init__(self, NT, stages, fused):
        self.NT = NT
        self.TOK = NT * TL
        self.stages = stages
        self.fused = fused
        nc = self.nc = bass.Bass("TRN2", target_bir_lowering=False)
        self.tk = TK(nc)
        self.dram = {}
        self.build()

    def din(self, name, shape, dt=F32):
        t = self.nc.dram_tensor(name, list(shape), dt, kind="ExternalInput")
        v = V(Buf(name), t.ap())
        self.dram[name] = v
        return v

    def dout(self, name, shape, dt=F32):
        t = self.nc.dram_tensor(name, list(shape), dt, kind="ExternalOutput")
        v = V(Buf(name), t.ap())
        self.dram[name] = v
        return v

    def dint(self, name, shape, dt=F32):
        t = self.nc.dram_tensor(name, list(shape), dt, kind="Internal")
        return V(Buf(name), t.ap())

    def sb(self, name, shape, dt=F32):
        return V(Buf(name), self.nc.alloc_sbuf_tensor(name, list(shape), dt).ap())

    def mm(self, ps, lhsT, rhs, start, stop):
        self.tk.op("pe", lambda E: E.matmul(ps.ap, lhsT=lhsT.ap, rhs=rhs.ap, start=start, stop=stop), [lhsT, rhs], [ps])

    def tr(self, ps, in_):
        self.tk.op("pe", lambda E: E.transpose(ps.ap, in_.ap, self.ident.ap), [in_, self.ident], [ps])

    def act(self, out, in_, func, bias=None, scale=None):
        reads = [in_]
        kw = {}
        if bias is not None:
            if isinstance(bias, V):
                reads.append(bias)
                kw["bias"] = bias.ap
            else:
                kw["bias"] = float(bias)
        if scale is not None:
            if isinstance(scale, V):
                reads.append(scale)
                kw["scale"] = scale.ap
            else:
                kw["scale"] = float(scale)
        tbl = "A" if func in (AF.Exp, AF.Ln) else ("B" if func in (AF.Sigmoid, AF.Silu, AF.Gelu_apprx_tanh) else None)
        self.tk.op("act", lambda E: E.activation(out.ap, in_.ap, func, **kw), reads, [out], tbl=tbl)

    def cbias(self, val):
        return self.constv.ap[:, self.cidx[val]:self.cidx[val] + 1]

    def tt(self, out, a, b, op, e="dve"):
        self.tk.op(e, lambda E: E.tensor_tensor(out.ap, a.ap, b.ap, op), [a, b], [out])

    def ts(self, out, in0, s1, s2, op0, op1=None, e="dve"):
        reads = [in0]
        a1 = s1.ap if isinstance(s1, V) else float(s1)
        if isinstance(s1, V):
            reads.append(s1)
        a2 = None
        if s2 is not None:
            a2 = s2.ap if isinstance(s2, V) else float(s2)
            if isinstance(s2, V):
                reads.append(s2)
        if op1 is None:
            self.tk.op(e, lambda E: E.tensor_scalar(out.ap, in0.ap, a1, None, op0), reads, [out])
        else:
            self.tk.op(e, lambda E: E.tensor_scalar(out.ap, in0.ap, a1, a2, op0, op1), reads, [out])

    def stt(self, out, in0, scalar, in1, op0, op1):
        reads = [in0, in1]
        a = scalar.ap if isinstance(scalar, V) else float(scalar)
        if isinstance(scalar, V):
            reads.append(scalar)
        self.tk.op("dve", lambda E: E.scalar_tensor_tensor(out.ap, in0.ap, a, in1.ap, op0, op1), reads, [out])

    def cp(self, out, in_, e="dve"):
        if e == "act":
            self.tk.op("act", lambda E: E.copy(out.ap, in_.ap), [in_], [out])
        else:
            self.tk.op(e, lambda E: E.tensor_copy(out.ap, in_.ap), [in_], [out])

    def dma(self, out, in_, key, q="sp", pace=()):
        self.tk.dma(q, out, in_, key, pace)

    def pace(self, v, stride=1, burst=1):
        self.pace_cnt += 1
        if self.pace_cnt % stride:
            return
        for _ in range(burst):
            if not self.conv_pending:
                return
            l, b = self.conv_pending.pop(0)
            self.emit_conv(l, [b], pace=[v])

    def flush_conv(self, l):
        rest = [x for x in self.conv_pending if x[0] == l]
        self.conv_pending = [x for x in self.conv_pending if x[0] != l]
        for (l2, b) in rest:
            self.emit_conv(l2, [b])

    def rsqrt_(self, out, in_, eps=EPS):
        self.act(out, in_, AF.Ln, bias=eps)
        self.act(out, out, AF.Exp, scale=-0.5)

    def ws_init(self, seq):
        self.wseq = seq
        self.wpos = 0
        self.wiss = 0
        self.wpending = []

    def ws_issue(self):
        l, b = self.wseq[self.wiss]
        slot = self.wslots.get()
        if b >= NBLK:
            self.dma(slot[:, :3968], V(self.wbf[l][b].buf, self.wbf[l][b].ap[:, :3968]), "w%d" % (self.wiss % 4))
        else:
            self.dma(slot, self.wbf[l][b], "w%d" % (self.wiss % 4))
        self.wpending.append(slot)
        self.wiss += 1

    def ws_next(self, l, b, depth=2):
        assert self.wseq[self.wpos] == (l, b), (self.wpos, self.wseq[self.wpos], (l, b))
        while self.wiss < len(self.wseq) and self.wiss <= self.wpos + depth and (self.wslots.free or self.wiss <= self.wpos):
            self.ws_issue()
        slot = self.wpending.pop(0)
        self.wpos += 1
        return slot

    def ws_free(self, slot):
        self.wslots.put(slot)

    def build(self):
        nc, tk = self.nc, self.tk
        NT, TOK = self.NT, self.TOK
        stages = self.stages
        layers = sorted({l for (_, l) in stages})
        self.flags_d = self.din("flags", [128, 8])
        self.cvec_d = self.din("cvec", [128, 16])
        self.wblk, self.wadn, self.wada, self.pvec_d, self.aup_d, self.wsT_d, self.pbc_d = {}, {}, {}, {}, {}, {}, {}
        self.wbf = {}
        for l in layers:
            self.wblk[l] = self.din("wblk%d" % l, [NBLK, 128, 4096])
            self.wadn[l] = self.din("wadn%d" % l, [128, 256])
            self.wada[l] = self.din("wada%d" % l, [12, 128, 4096])
            self.pvec_d[l] = self.din("pvec%d" % l, [128, NPV])
            self.aup_d[l] = self.din("aup%d" % l, [16, 1024])
            self.wsT_d[l] = self.din("wsT%d" % l, [128, 1024])
            self.pbc_d[l] = self.din("pbc%d" % l, [128, 3072])
            t = nc.dram_tensor("wbf%d" % l, [NBLK + 8, 128, 4096], BF16, kind="Internal").ap()
            self.wbf[l] = [V(Buf("wbf%d_%d" % (l, b)), t[b]) for b in range(NBLK + 8)]
        first, last = stages[0], stages[-1]
        self.hsrc, self.hdst, self.csrc, self.cdst = {}, {}, {}, {}
        if ("P1", 0) in stages or ("P2", 0) in stages:
            self.hsrc[0] = self.din("xT", [D, TOK])
            self.csrc[0] = self.din("ctxT", [D, CTX])
        if ("P2", 0) in stages:
            if ("P2", 1) in stages:
                h1 = self.dint("h1", [D, TOK])
                c1 = self.dint("hctx1", [D, CTX])
            else:
                h1 = self.dout("h1", [D, TOK])
                c1 = self.dout("hctx1", [D, CTX])
            self.hdst[0] = h1
            self.cdst[0] = c1
            self.hsrc[1] = h1
            self.csrc[1] = c1
        elif ("P1", 1) in stages or ("P2", 1) in stages:
            self.hsrc[1] = self.din("h1", [D, TOK])
            if ("P1", 1) in stages:
                self.csrc[1] = self.din("hctx1", [D, CTX])
        if ("P2", 1) in stages:
            self.hdst[1] = self.dout("outT", [D, TOK])
        self.rec_mine, self.rec_all, self.sbs = {}, {}, {}
        PW = [1024, 1024, 1024, 1024, 128]
        for l in layers:
            if ("P1", l) in stages:
                if self.fused:
                    self.rec_mine[l] = [self.dint("rec%d_%d" % (l, p), [128, PW[p]]) for p in range(5)]
                else:
                    t = self.dout("st_out", [128, NREC])
                    self.rec_mine[l] = [V(Buf("st_out%d" % p), t.ap[:, p * 1024:min(NREC, (p + 1) * 1024)]) for p in range(5)]
            if ("P2", l) in stages:
                if self.fused:
                    self.rec_all[l] = [self.dint("recall%d_%d" % (l, p), [4 * 128, PW[p]]) for p in range(5)]
                else:
                    t = self.din("st_in", [4 * 128, NREC])
                    self.rec_all[l] = [V(Buf("st_in%d" % p), t.ap[:, p * 1024:min(NREC, (p + 1) * 1024)]) for p in range(5)]
                self.sbs[l] = None
        self.gsc, self.gdec = {}, {}
        for l in layers:
            if REUSE and self.fused and ("P1", l) in stages and ("P2", l) in stages:
                ne = (NT + 1) * 4
                t = nc.dram_tensor("gsc%d" % l, [ne, 128, 3072], BF16, kind="Internal").ap()
                self.gsc[l] = [V(Buf("gsc%d_%d" % (l, i)), t[i]) for i in range(ne)]
                t = nc.dram_tensor("gdec%d" % l, [ne, 128, 8], F32, kind="Internal").ap()
                self.gdec[l] = [V(Buf("gdec%d_%d" % (l, i)), t[i]) for i in range(ne)]
        for l in layers:
            if ("P2", l) in stages:
                if ("P1", l) in stages:
                    t = nc.dram_tensor("sbs%d" % l, [NT, 128, 1028], F32, kind="Internal").ap()
                    self.sbs[l] = [V(Buf("sbs%d_%d" % (l, n)), t[n]) for n in range(NT)]
                else:
                    t = self.din("sbs_in", [NT * 128, 1028])
                    self.sbs[l] = [V(Buf("sbs_in%d" % n), t.ap[n * 128:(n + 1) * 128, :]) for n in range(NT)]
            elif ("P1", l) in stages:
                t = self.dout("sbs_out", [NT * 128, 1028])
                self.sbs[l] = [V(Buf("sbs_out%d" % n), t.ap[n * 128:(n + 1) * 128, :]) for n in range(NT)]

        sb = self.sb
        self.ident = sb("ident", [128, 128], BF16)
        self.identf = sb("identf", [128, 128])
        self.hcp = [sb("hcp%d" % i, [128, 752], BF16) for i in range(2)]
        self.pad_mode = None
        self.hcp_i = 0
        self.onesf = sb("onesf", [128, 128])
        self.onesD = sb("onesD", [128, 128])
        self.maskf = sb("maskf", [128, 512], BF16)
        self.maskb = sb("maskb", [128, 512], BF16)
        self.constv = sb("constv", [128, 4])
        self.cidx = {EPS: 0, 1.0: 1, float(np.log(128.0 ** -0.5)): 2, 0.0: 3}
        self.LNSC = float(np.log(128.0 ** -0.5))
        self.flags = sb("flags_s", [128, 8])
        self.cvec = sb("cvec_s", [128, 16])
        self.LP = {}
        for l in layers:
            self.LP[l] = dict(pvec=sb("pvec_s%d" % l, [128, NPV]), negab=sb("negab%d" % l, [128, 8]), adaA=sb("adaA%d" % l, [128, 2, 16]), adaB=sb("adaB%d" % l, [128, 2, 32]),
                              A1=sb("A1_%d" % l, [128, 2, 8]), A2=sb("A2_%d" % l, [128, 2, 8]), aupb=sb("aupb%d" % l, [16, 1024], BF16),
                              adnw=sb("adnw%d" % l, [128, 256], BF16))
        self.wsTb = sb("wsTb", [128, 1024], BF16)
        self.pbc = sb("pbc", [128, 3072])
        self.onesDb = sb("onesDb", [128, 128], BF16)
        self.ones256b = sb("ones256b", [128, 128], BF16)
        self.Sf = [sb("Sf%d" % h, [128, 256]) for h in range(4)]
        self.Sb = [sb("Sb%d" % h, [128, 256]) for h in range(4)]
        self.Sbin = [sb("Sbin%d" % h, [128, 256]) for h in range(4)]
        self.Sacc = [sb("Sacc%d" % h, [128, 256]) for h in range(4)]
        self.Tf = sb("Tf", [128, 256])
        self.Dsuf = sb("Dsuf", [128, 4])
        self.Abc = sb("Abc", [128, 4])
        self.small = Pool("small", [sb("sm%d" % i, [128, 16]) for i in range(12)])
        self.wslots = Pool("wslots", [sb("wslot%d" % i, [128, 4096], BF16) for i in range(4)])
        self.hTs = [[sb("hT%d_%d" % (k, c), [128, TL]) for c in range(NCH)] for k in range(2)]
        self.hT = self.hTs[0]
        self.h_loaded = None
        self.yacc = [sb("yacc%d" % c, [128, TL]) for c in range(NCH)]
        self.PF = Pool("PF", [sb("pf%d" % i, [128, 512]) for i in range(19)])
        self.PB = Pool("PB", [sb("pb%d" % i, [128, 512], BF16) for i in range(32)])
        self.Sbf = [[sb("Sbf%d_%d" % (e, c), [128, 256], BF16) for c in range(4)] for e in range(2)]
        self.PS = Pool("PS", [V(Buf("ps%d" % i), nc.alloc_psum_tensor("ps%d" % i, [128, 512], F32).ap()) for i in range(7)])
        self.pst = V(Buf("pst"), nc.alloc_psum_tensor("pst", [128, 1024], BF16).ap())

        P = "pool"
        tk.op(P, lambda E: E.memset(self.ident.ap, 1.0), [], [self.ident])
        tk.op(P, lambda E: E.affine_select(out=self.ident.ap, in_=self.ident.ap, pattern=[[-1, 128]], compare_op=ALU.is_equal, fill=0.0, base=0, channel_multiplier=1), [self.ident], [self.ident])
        tk.op(P, lambda E: E.memset(self.identf.ap, 1.0), [], [self.identf])
        tk.op(P, lambda E: E.affine_select(out=self.identf.ap, in_=self.identf.ap, pattern=[[-1, 128]], compare_op=ALU.is_equal, fill=0.0, base=0, channel_multiplier=1), [self.identf], [self.identf])
        tk.op(P, lambda E: E.memset(self.onesf.ap, 1.0), [], [self.onesf])
        tk.op(P, lambda E: E.memset(self.onesD.ap, 1.0 / D), [], [self.onesD])
        tk.op(P, lambda E: E.memset(self.onesDb.ap, 1.0 / D), [], [self.onesDb])
        tk.op(P, lambda E: E.memset(self.ones256b.ap, 1.0 / 256), [], [self.ones256b])
        tk.op(P, lambda E: E.memset(self.maskf.ap, 1.0), [], [self.maskf])
        tk.op(P, lambda E: E.memset(self.maskb.ap, 1.0), [], [self.maskb])
        for c in range(4):
            mf = self.maskf[:, c * 128:(c + 1) * 128]
            mb = self.maskb[:, c * 128:(c + 1) * 128]
            tk.op(P, lambda E, mf=mf: E.affine_select(out=mf.ap, in_=mf.ap, pattern=[[1, 128]], compare_op=ALU.is_ge, fill=0.0, base=0, channel_multiplier=-1), [mf], [mf])
            tk.op(P, lambda E, mb=mb: E.affine_select(out=mb.ap, in_=mb.ap, pattern=[[-1, 128]], compare_op=ALU.is_gt, fill=0.0, base=0, channel_multiplier=1), [mb], [mb])
        for val, i in self.cidx.items():
            cv = self.constv[:, i:i + 1]
            tk.op(P, lambda E, cv=cv, val=val: E.memset(cv.ap, val), [], [cv])
        self.dma(self.flags, self.flags_d, "misc")
        self.dma(self.cvec, self.cvec_d, "misc")
        self.scT = sb("scT", [128, 16])
        self.act(self.scT, self.cvec, AF.Silu)

        for l in layers:
            self.dma(self.LP[l]["adnw"], self.wadn[l], "adn%d" % l, q="pool")
        need = {}
        for (st, l) in stages:
            blks = [0, 1, 2, 3] if st == "P1" else [0, 1, 2, 3] + [b for b in P2_ORDER if b >= 4 and b < NBLK]
            cur = need.setdefault(l, [])
            for b in blks:
                if b not in cur:
                    cur.append(b)
        self.need = need
        self.conv_done = set()
        l0 = stages[0][1]
        self.emit_conv(l0, [0, 1, 2, 3])
        self.pace_cnt = 0
        self.conv_pending = [(l, b) for l in layers for b in need[l] if (l, b) not in self.conv_done]
        self.pool_dma_ok = False

        seq = []
        for (st, l) in stages:
            if st == "P1":
                seq += [(l, b) for _ in range(NT + 1) for b in range(4)]
            else:
                ntile = NT + (1 if l == 0 else 0)
                seq += [(l, b) for _ in range(ntile) for b in P2_ORDER]
        self.ws_init(seq)

        self.layer_setup(l0)
        self.setup_done = {l0}
        self.hsel = 0
        for (st, l) in stages:
            self.use_layer(l)
            if st == "P1":
                self.stage_p1(l)
            else:
                self.stage_p2(l)
        import time as _t
        _t0 = _t.time()
        tk.flush(SCHED_WINDOW)
        print("[kernel] scheduled %d instrs in %.1fs, est %.2f ms" % (tk.nins, _t.time() - _t0, getattr(tk, "sched_est", 0) / 1e6), flush=True)
        outs = [v.buf for k, v in self.dram.items() if k in ("outT", "h1", "hctx1", "st_out", "sbs_out")]
        extra = []
        for l in self.sbs:
            if self.sbs[l] and not self.fused:
                extra += [v.buf for v in self.sbs[l]]
        tk.wait_all("sp", outs + extra)

    def emit_conv(self, l, blks=None, pace=()):
        for b in (self.need[l] if blks is None else blks):
            if (l, b) in self.conv_done:
                continue
            self.conv_done.add((l, b))
            src = V(self.wblk[l].buf, self.wblk[l].ap[b].rearrange("p (a n) -> (p a) n", n=2048))
            dst = V(self.wbf[l][b].buf, self.wbf[l][b].ap.rearrange("p (a n) -> (p a) n", n=2048))
            self.dma(dst, src, "cv%d" % (b % 4), q="pool", pace=pace)

    def late_plan(self, cur):
        items = []
        for l in sorted(self.need):
            if l in self.setup_done:
                continue
            self.setup_done.add(l)
            items.append(lambda l=l: (self.layer_setup(l, ada=False), self.use_layer(cur)))
            for j in range(12):
                items.append(lambda l=l, j=j: self.ada_piece(l, j))
            items.append(lambda l=l: self.ada_finish(l))
        return items

    def use_layer(self, l):
        for k, v in self.LP[l].items():
            setattr(self, k, v)

    def setup_late(self, l):
        wtmp = [self.PF.get(), self.PF.get()]
        for i in range(2):
            self.dma(wtmp[i], V(self.wsT_d[l].buf, self.wsT_d[l].ap[:, i * 512:(i + 1) * 512]), "misc%d" % i)
            self.cp(self.wsTb[:, i * 512:(i + 1) * 512], wtmp[i])
        self.PF.put(*wtmp)
        self.dma(self.pbc, self.pbc_d[l], "misc2")

    def layer_setup(self, l, ada=True):
        self.use_layer(l)
        pv = self.pvec
        self.dma(self.pvec, self.pvec_d[l], "misc0")
        wtmp = [self.PF.get(), self.PF.get()]
        for i in range(2):
            self.dma(wtmp[i][:16, :], V(self.aup_d[l].buf, self.aup_d[l].ap[:, i * 512:(i + 1) * 512]), "misc%d" % (1 + i))
            self.cp(self.aupb[:, i * 512:(i + 1) * 512], wtmp[i][:16, :])
        self.PF.put(*wtmp)
        self.ts(self.negab, pv[:, P_AB:P_AB + 8], -1.0, None, ALU.mult)
        if ("P2", l) in self.stages:
            for c in range(NCH):
                slot = self.wslots.get()
                s3 = slot[:, :3968].re("p (j n) -> p j n", n=128)
                for j in range(31):
                    self.ts(s3[:, j, :], self.identf, pv[:, P_CW + c * 31 + j:P_CW + c * 31 + j + 1], None, ALU.mult)
                self.dma(V(self.wbf[l][NBLK + c].buf, self.wbf[l][NBLK + c].ap[:, :3968]), slot[:, :3968], "dg%d" % (c % 2))
                self.wslots.put(slot)
        if ada:
            for j in range(12):
                self.ada_piece(l, j)
            self.ada_finish(l)

    def ada_piece(self, l, j):
        LPl = self.LP[l]
        pv = LPl["pvec"]
        ps = self.PS.get()
        wt = [self.PF.get() for _ in range(8)]
        for kc in range(8):
            self.dma(wt[kc], V(self.wada[l].buf, self.wada[l].ap[j][:, kc * 512:(kc + 1) * 512]), "ada%d" % kc)
        for m in range(4):
            col = m * 2
            for kc in range(8):
                self.mm(ps[:, col:col + 2], wt[kc][:, m * 128:(m + 1) * 128], self.scT[:, kc * 2:kc * 2 + 2], kc == 0, kc == 7)
        self.PF.put(*wt)
        for w in range(2):
            dst = LPl["adaA"][:, w, 4 * j:4 * j + 4] if j < 4 else LPl["adaB"][:, w, 4 * (j - 4):4 * (j - 4) + 4]
            self.tt(dst, ps[:, w:8:2], pv[:, P_BADA + 4 * j:P_BADA + 4 * j + 4], ALU.add)
        self.PS.put(ps)
        if j == 3:
            for w in range(2):
                self.stt(LPl["A1"][:, w, :], LPl["adaA"][:, w, 8:16], 1.0, pv[:, P_N1G:P_N1G + 8], ALU.add, ALU.mult)

    def ada_finish(self, l):
        LPl = self.LP[l]
        pv = LPl["pvec"]
        for w in range(2):
            self.stt(LPl["A2"][:, w, :], LPl["adaB"][:, w, 16:24], 1.0, pv[:, P_N2G:P_N2G + 8], ALU.add, ALU.mult)

    def prefetch_h(self, src, t0, T):
        k = 1 - self.hsel
        q = "pool" if self.pool_dma_ok else "sp"
        for c in range(NCH):
            self.dma(self.hTs[k][c][:, :T], V(src.buf, src.ap[c * 128:(c + 1) * 128, t0:t0 + T]), "h%d" % c, q=q)
        self.h_loaded = (id(src.buf), t0, T)

    def load_h(self, src, t0, T):
        if self.h_loaded != (id(src.buf), t0, T):
            self.prefetch_h(src, t0, T)
        self.hsel = 1 - self.hsel
        self.hT = self.hTs[self.hsel]
        self.h_loaded = None

    def norm_mod(self, T, A, Bv, out_bf=True, outs=None):
        ps = self.PS.get()
        for c in range(NCH):
            sq = self.PB.get()
            self.act(sq[:, :T], self.hT[c][:, :T], AF.Square)
            self.mm(ps[:, :T], self.onesDb, sq[:, :T], c == 0, c == NCH - 1)
            self.PB.put(sq)
        rstd = self.PF.get()
        self.rsqrt_(rstd[:, :T], ps[:, :T])
        self.PS.put(ps)
        res = []
        for c in range(NCH):
            tmp = self.PF.get()
            self.tt(tmp[:, :T], self.hT[c][:, :T], rstd[:, :T], ALU.mult)
            if out_bf:
                o = self.PB.get()
            else:
                o = outs[c]
            if Bv is None:
                self.ts(o[:, :T], tmp[:, :T], A[:, c:c + 1], None, ALU.mult)
            else:
                self.act(o[:, :T], tmp[:, :T], AF.Identity, bias=Bv[:, c:c + 1], scale=A[:, c:c + 1])
            self.PF.put(tmp)
            res.append(o)
        self.PF.put(rstd)
        return res

    def proj_fm(self, ps, slot, hl, m, T, ncols=512, cw=128):
        w3 = slot.re("p (k n) -> p k n", n=ncols)
        for kc in range(NCH):
            self.mm(ps[:cw, :T] if cw < 128 else ps[:, :T], w3[:, kc, m * cw:(m + 1) * cw], hl[kc][:, :T], kc == 0, kc == NCH - 1)

    def adn_proj(self, hl, T):
        w3 = self.adnw.re("p (k n) -> p k n", n=32)
        res = []
        for e in range(2):
            ps = self.PS.get()
            for kc in range(NCH):
                self.mm(ps[:16, :T], w3[:, kc, e * 16:(e + 1) * 16], hl[kc][:, :T], kc == 0, kc == NCH - 1)
            o = self.PB.get()
            self.cp(o[:16, :T], ps[:16, :T], e="act")
            self.PS.put(ps)
            res.append(o)
        return res

    def gla_head_prep(self, h, slot, hl, adn, T, need_q, l=None, idx=None):
        nchk = T // 128
        R = {}
        reuse = l is not None and l in self.gsc
        if reuse and need_q:
            return self.gla_head_prep_p2(h, slot, hl, adn, T, self.gsc[l][idx], self.gdec[l][idx])
        store = reuse and not need_q
        Gq, Gk, dec = [None, None], [None, None], [None, None]
        for e in range(2):
            ps = self.PS.get()
            self.mm(ps[:, :T], self.aupb[:, e * 512 + h * 128:e * 512 + (h + 1) * 128], adn[e][:16, :T], True, True)
            sp = self.PF.get()
            self.act(sp[:, :T], ps[:, :T], AF.Exp, bias=self.negab[:, e * 4 + h:e * 4 + h + 1], scale=-1.0)
            self.PS.put(ps)
            self.act(sp[:, :T], sp[:, :T], AF.Ln, bias=1.0)
            cs = self.PF.get()
            for c in range(nchk):
                sl = slice(c * 128, (c + 1) * 128)
                self.tk.op("dve", lambda E, cs=cs, sl=sl, sp=sp: E.tensor_tensor_scan(cs.ap[:, sl], self.onesf.ap, sp.ap[:, sl], 0.0, ALU.mult, ALU.add), [self.onesf, sp], [cs])
            dec[e] = self.small.get()
            if e == 0:
                self.act(dec[e][:, :nchk], cs[:, 127:T:128], AF.Exp, scale=-1.0 / 16)
                bsrc, sgn = cs, -1.0
            else:
                for c in range(nchk):
                    sl = slice(c * 128, (c + 1) * 128)
                    self.stt(sp[:, sl], cs[:, sl], cs[:, c * 128 + 127:c * 128 + 128], sp[:, sl], ALU.subtract, ALU.subtract)
                self.act(dec[e][:, :nchk], sp[:, 0:T:128], AF.Exp, scale=1.0 / 16)
                bsrc, sgn = sp, 1.0
            if need_q:
                Gq[e] = self.PF.get()
                self.act(Gq[e][:, :T], bsrc[:, :T], AF.Exp, bias=self.LNSC, scale=sgn / 16)
            Gk[e] = self.PF.get()
            self.act(Gk[e][:, :T], bsrc[:, :T], AF.Exp, scale=-sgn / 16)
            self.PF.put(sp, cs)
        if need_q:
            ps = self.PS.get()
            self.proj_fm(ps, slot, hl, 0, T)
            R["qe"] = []
            for e in range(2):
                o = self.PB.get()
                self.tt(o[:, :T], ps[:, :T], Gq[e][:, :T], ALU.mult)
                self.PF.put(Gq[e])
                R["qe"].append(o)
            self.PS.put(ps)
        ps = self.PS.get()
        self.proj_fm(ps, slot, hl, 1, T)
        R["ke"] = []
        kdfm = []
        for e in range(2):
            k32 = self.PF.get()
            self.tt(k32[:, :T], ps[:, :T], Gk[e][:, :T], ALU.mult)
            self.PF.put(Gk[e])
            if need_q or store:
                o = self.PB.get()
                self.cp(o[:, :T], k32[:, :T], e="act")
                R["ke"].append(o)
            kd = self.PB.get()
            for c in range(nchk):
                sl = slice(c * 128, (c + 1) * 128)
                self.act(kd[:, sl], k32[:, sl], AF.Identity, scale=dec[e][:, c:c + 1])
            self.PF.put(k32)
            kdfm.append(kd)
        self.PS.put(ps)
        for e in range(2):
            for c in range(nchk):
                col = (e * 4 + c) * 128
                self.tr(self.pst[:, col:col + 128], kdfm[e][:, c * 128:(c + 1) * 128])
        self.kd_t = [self.PB.get(), self.PB.get()]
        self.vt_t = [self.PB.get(), self.PB.get()]
        for e in range(2):
            self.cp(self.kd_t[e][:, :nchk * 128], self.pst[:, e * 512:e * 512 + nchk * 128], e="act")
        self.PB.put(*kdfm)
        w3 = slot.re("p (k n) -> p k n", n=512)
        for c0 in range(0, nchk, 2):
            ps = self.PS.get()
            for c in range(c0, c0 + 2):
                for kc in range(NCH):
                    self.mm(ps[:, (c - c0) * 256:(c - c0 + 1) * 256], hl[kc][:, c * 128:(c + 1) * 128], w3[:, kc, 256:512], kc == 0, kc == NCH - 1)
            self.cp(self.vt_t[c0 // 2], ps[:, :], e="act")
            self.PS.put(ps)
        R["dec"] = dec
        if store:
            g, gd = self.gsc[l][idx], self.gdec[l][idx]
            for e in range(2):
                self.dma(V(g.buf, g.ap[:, e * 512:e * 512 + T]), R["ke"][e][:, :T], "gs%d" % e)
                self.dma(V(g.buf, g.ap[:, 1024 + e * 512:1024 + e * 512 + nchk * 128]), self.kd_t[e][:, :nchk * 128], "gs%d" % (2 + e))
                self.dma(V(gd.buf, gd.ap[:, e * 4:e * 4 + nchk]), dec[e][:, :nchk], "gs%d" % (5 + e))
            for c0 in range(0, nchk, 2):
                self.dma(V(g.buf, g.ap[:, 2048 + c0 * 256:2048 + (c0 + 2) * 256]), self.vt_t[c0 // 2], "gs%d" % (4 if c0 == 0 else 7))
            self.PB.put(*R["ke"])
            R["ke"] = []
        return R

    def gla_head_prep_p2(self, h, slot, hl, adn, T, g, gd):
        nchk = T // 128
        R = {"qe": [], "ke": [], "dec": []}
        Gq = [None, None]
        for e in range(2):
            ps = self.PS.get()
            self.mm(ps[:, :T], self.aupb[:, e * 512 + h * 128:e * 512 + (h + 1) * 128], adn[e][:16, :T], True, True)
            sp = self.PF.get()
            self.act(sp[:, :T], ps[:, :T], AF.Exp, bias=self.negab[:, e * 4 + h:e * 4 + h + 1], scale=-1.0)
            self.PS.put(ps)
            self.act(sp[:, :T], sp[:, :T], AF.Ln, bias=1.0)
            cs = self.PF.get()
            for c in range(nchk):
                sl = slice(c * 128, (c + 1) * 128)
                self.tk.op("dve", lambda E, cs=cs, sl=sl, sp=sp: E.tensor_tensor_scan(cs.ap[:, sl], self.onesf.ap, sp.ap[:, sl], 0.0, ALU.mult, ALU.add), [self.onesf, sp], [cs])
            if e == 0:
                bsrc, sgn = cs, -1.0
            else:
                for c in range(nchk):
                    sl = slice(c * 128, (c + 1) * 128)
                    self.stt(sp[:, sl], cs[:, sl], cs[:, c * 128 + 127:c * 128 + 128], sp[:, sl], ALU.subtract, ALU.subtract)
                bsrc, sgn = sp, 1.0
            Gq[e] = self.PF.get()
            self.act(Gq[e][:, :T], bsrc[:, :T], AF.Exp, bias=self.LNSC, scale=sgn / 16)
            self.PF.put(sp, cs)
        ps = self.PS.get()
        self.proj_fm(ps, slot, hl, 0, T)
        for e in range(2):
            o = self.PB.get()
            self.tt(o[:, :T], ps[:, :T], Gq[e][:, :T], ALU.mult)
            self.PF.put(Gq[e])
            R["qe"].append(o)
        self.PS.put(ps)
        q = "pool" if self.pool_dma_ok else "sp"
        self.kd_t = [self.PB.get(), self.PB.get()]
        self.vt_t = [self.PB.get(), self.PB.get()]
        for e in range(2):
            o = self.PB.get()
            self.dma(o[:, :T], V(g.buf, g.ap[:, e * 512:e * 512 + T]), "gl%d" % e, q=q)
            R["ke"].append(o)
            self.dma(self.kd_t[e][:, :nchk * 128], V(g.buf, g.ap[:, 1024 + e * 512:1024 + e * 512 + nchk * 128]), "gl%d" % (2 + e), q=q)
            d = self.small.get()
            self.dma(d[:, :nchk], V(gd.buf, gd.ap[:, e * 4:e * 4 + nchk]), "gl%d" % (5 + e), q=q)
            R["dec"].append(d)
        for c0 in range(0, nchk, 2):
            self.dma(self.vt_t[c0 // 2], V(g.buf, g.ap[:, 2048 + c0 * 256:2048 + (c0 + 2) * 256]), "gl%d" % (4 if c0 == 0 else 7), q=q)
        return R

    def kv(self, ps, e, c):
        self.mm(ps, self.kd_t[e][:, c * 128:(c + 1) * 128], self.vt_t[c // 2][:, (c % 2) * 256:(c % 2 + 1) * 256], True, True)

    def free_kdvt(self):
        self.PB.put(*self.kd_t)
        self.PB.put(*self.vt_t)

    def p1_tile(self, l, src, t0, T, which, is_ctx, n, nxt=None):
        nchk = T // 128
        self.load_h(src, t0, T)
        if nxt is not None:
            self.prefetch_h(*nxt)
        hl = self.norm_mod(T, self.A1[:, which, :], self.adaA[:, which, 0:8])
        adn = self.adn_proj(hl, T)
        if not is_ctx:
            for h in range(4):
                self.dma(V(self.sbs[l][n].buf, self.sbs[l][n].ap[:, h * 256:(h + 1) * 256]), self.Sb[h], "sbs%d" % h)
            self.dma(V(self.sbs[l][n].buf, self.sbs[l][n].ap[:, 1024:1028]), self.Abc, "sbs4")
        for h in range(4):
            slot = self.ws_next(l, B_QKV + h)
            R = self.gla_head_prep(h, slot, hl, adn, T, False, l, (self.NT if is_ctx else n) * 4 + h)
            self.ws_free(slot)
            dec = R["dec"]
            for c in range(nchk - 1, -1, -1):
                ps = self.PS.get()
                self.kv(ps[:, :256], 1, c)
                self.stt(self.Sb[h], self.Sb[h], dec[1][:, c:c + 1], ps[:, :256], ALU.mult, ALU.add)
                self.PS.put(ps)
                self.tt(self.Abc[:, h:h + 1], self.Abc[:, h:h + 1], dec[1][:, c:c + 1], ALU.mult, e="pool")
            df = self.small.get()
            for c in range(nchk):
                ps = self.PS.get()
                self.kv(ps[:, :256], 0, c)
                if c == 0:
                    self.cp(self.Tf, ps[:, :256])
                    self.cp(df[:, 0:1], dec[0][:, 0:1], e="pool")
                else:
                    self.stt(self.Tf, self.Tf, dec[0][:, c:c + 1], ps[:, :256], ALU.mult, ALU.add)
                    self.tt(df[:, 0:1], df[:, 0:1], dec[0][:, c:c + 1], ALU.mult, e="pool")
                self.PS.put(ps)
            self.stt(self.Sacc[h], self.Tf, self.Dsuf[:, h:h + 1], self.Sacc[h], ALU.mult, ALU.add)
            if l == self.stages[0][1]:
                self.pace(self.Sacc[h], 1, 3)
            self.tt(self.Dsuf[:, h:h + 1], self.Dsuf[:, h:h + 1], df[:, 0:1], ALU.mult)
            self.small.put(df, dec[0], dec[1])
            self.free_kdvt()
        self.PB.put(*hl)
        self.PB.put(*adn)

    def p1_reset(self):
        P = "dve"
        for h in range(4):
            self.tk.op(P, lambda E, h=h: E.memset(self.Sb[h].ap, 0.0), [], [self.Sb[h]])
            self.tk.op(P, lambda E, h=h: E.memset(self.Sacc[h].ap, 0.0), [], [self.Sacc[h]])
        self.tk.op(P, lambda E: E.memset(self.Dsuf.ap, 1.0), [], [self.Dsuf])
        self.tk.op(P, lambda E: E.memset(self.Abc.ap, 1.0), [], [self.Abc])

    def stage_p1(self, l):
        recs = self.rec_mine[l]

        def recv(c0, c1):
            p = c0 // 1024
            return V(recs[p].buf, recs[p].ap[:, c0 - p * 1024:c1 - p * 1024])
        self.p1_reset()
        self.p1_tile(l, self.csrc[l], 0, CTX, 1, True, None, nxt=(self.hsrc[l], (self.NT - 1) * TL, TL))
        for h in range(4):
            self.dma(recv(2048 + h * 256, 2048 + (h + 1) * 256), self.Sacc[h], "rec%d" % h)
            self.dma(recv(3072 + h * 256, 3072 + (h + 1) * 256), self.Sb[h], "rec%d" % (4 + h))
        self.p1_reset()
        for n in range(self.NT - 1, -1, -1):
            self.p1_tile(l, self.hsrc[l], n * TL, TL, 0, False, n, nxt=((self.hsrc[l], (n - 1) * TL, TL) if n > 0 else None))
        for h in range(4):
            self.dma(recv(h * 256, (h + 1) * 256), self.Sacc[h], "rec%d" % h)
            self.dma(recv(1024 + h * 256, 1024 + (h + 1) * 256), self.Sb[h], "rec%d" % (4 + h))
        zt = self.PF.get()
        self.tk.op("dve", lambda E, zt=zt: E.memset(zt.ap[:, :128], 0.0), [], [zt])
        self.dma(recs[4], zt[:, :128], "rec8")
        self.PF.put(zt)
        self.dma(recv(4096, 4100), self.Dsuf, "rec8")
        self.dma(recv(4100, 4104), self.Abc, "rec9")

    def stage_x(self, l):
        ras = self.rec_all[l]
        if self.fused:
            for p in range(5):
                mine, ra = self.rec_mine[l][p], ras[p]

                def emit(tk, mine=mine, ra=ra):
                    tk._wait("pool", tk._deps([mine], [ra]))
                    ins = self.nc.gpsimd.collective_compute("AllGather", ALU.bypass, replica_groups=[[0, 1, 2, 3], [4, 5, 6, 7]], ins=[mine.ap.opt()], outs=[ra.ap.opt()])
                    sem = tk._sem("cc")
                    tk.cnt["cc"] += 1
                    ins.then_inc(sem, 1)
                    self.nc.gpsimd.wait_ge(sem, tk.cnt["cc"])
                    tk.seen["pool"]["cc"] = tk.cnt["cc"]
                    ra.buf.w = ("cc", tk.cnt["cc"])
                    ra.buf.r = {}
                    mine.buf.r["cc"] = tk.cnt["cc"]

                self.tk.custom("pool", emit, [mine], [ra], 40000.0)

        def rv(i, c0, c1):
            p = c0 // 1024
            return V(ras[p].buf, ras[p].ap[i * 128:(i + 1) * 128, c0 - p * 1024:c1 - p * 1024])

        for (dst, base, ctxo, dco, order, fo) in ((self.Sf, 0, 2048, 4096, range(4), 0), (self.Sbin, 1024, 3072, 4100, range(3, -1, -1), 4)):
            for h in range(4):
                self.dma(dst[h], rv(0, ctxo + h * 256, ctxo + (h + 1) * 256), "xs%d" % h)
            for i in order:
                ai = self.small.get()
                self.dma(ai[:, 0:4], rv(i, dco, dco + 4), "xsa")
                ae = self.small.get()
                self.ts(ae[:, 0:4], ai[:, 0:4], 1.0, self.flags[:, fo + i:fo + i + 1], ALU.subtract, ALU.mult)
                self.ts(ae[:, 0:4], ae[:, 0:4], 1.0, None, ALU.add)
                for h in range(4):
                    sl = self.PF.get()
                    self.dma(sl[:, :256], rv(i, base + h * 256, base + (h + 1) * 256), "xs%d" % h)
                    self.ts(sl[:, :256], sl[:, :256], self.flags[:, fo + i:fo + i + 1], None, ALU.mult)
                    self.stt(dst[h], dst[h], ae[:, h:h + 1], sl[:, :256], ALU.mult, ALU.add)
                    self.PF.put(sl)
                self.small.put(ai, ae)

    def gated_out(self, l, gblk, oblk, xin, first):
        T = self.T
        for half in range(2):
            gs = self.ws_next(l, gblk + half)
            os_ = self.ws_next(l, oblk + half)
            o3 = os_.re("p (k n) -> p k n", n=512)
            for m in range(4):
                mc = half * 4 + m
                psg = self.PS.get()
                self.proj_fm(psg, gs, self.hl, m, T)
                sig = self.PF.get()
                self.act(sig[:, :T], psg[:, :T], AF.Sigmoid)
                self.PS.put(psg)
                ps = self.PS.get()
                for kc in range(NCH):
                    self.mm(ps[:, :T], o3[:, kc, m * 128:(m + 1) * 128], xin[kc][:, :T], kc == 0, kc == NCH - 1)
                if first:
                    self.tt(self.yacc[mc][:, :T], ps[:, :T], sig[:, :T], ALU.mult)
                else:
                    self.tt(sig[:, :T], ps[:, :T], sig[:, :T], ALU.mult)
                    self.tt(self.yacc[mc][:, :T], self.yacc[mc][:, :T], sig[:, :T], ALU.add)
                self.PS.put(ps)
                self.PF.put(sig)
            self.ws_free(gs)
            self.ws_free(os_)

    def p2_tile(self, l, src, dst, t0, T, which, is_ctx, n, final, nxt=None):
        self.T = T
        self.dbg_on = (not is_ctx) and n == 0 and l == 0
        nchk = T // 128
        pv = self.pvec
        self.load_h(src, t0, T)
        hl = self.hl = self.norm_mod(T, self.A1[:, which, :], self.adaA[:, which, 0:8])
        adn = self.adn_proj(hl, T)
        self.dbg("hl", hl, T)
        if is_ctx:
            for h in range(4):
                self.tk.op("dve", lambda E, h=h: E.memset(self.Sf[h].ap, 0.0), [], [self.Sf[h]])
                self.tk.op("dve", lambda E, h=h: E.memset(self.Sb[h].ap, 0.0), [], [self.Sb[h]])
        else:
            ab = self.small.get()
            self.dma(ab[:, 0:4], V(self.sbs[l][n].buf, self.sbs[l][n].ap[:, 1024:1028]), "sbl")
            for h in range(4):
                self.dma(self.Sb[h], V(self.sbs[l][n].buf, self.sbs[l][n].ap[:, h * 256:(h + 1) * 256]), "sbl%d" % h)
                self.stt(self.Sb[h], self.Sbin[h], ab[:, h:h + 1], self.Sb[h], ALU.mult, ALU.add)
            self.small.put(ab)
        on32 = []
        on = []
        for h in range(4):
            slot = self.ws_next(l, B_QKV + h)
            R = self.gla_head_prep(h, slot, hl, adn, T, True, l, (self.NT if is_ctx else n) * 4 + h)
            self.ws_free(slot)
            dec, qe, ke = R["dec"], R["qe"], R["ke"]
            for c in range(nchk):
                self.cp(self.Sbf[0][c], self.Sf[h], e="act")
                ps = self.PS.get()
                self.kv(ps[:, :256], 0, c)
                self.stt(self.Sf[h], self.Sf[h], dec[0][:, c:c + 1], ps[:, :256], ALU.mult, ALU.add)
                self.PS.put(ps)
            for c in range(nchk - 1, -1, -1):
                self.cp(self.Sbf[1][c], self.Sb[h], e="act")
                ps = self.PS.get()
                self.kv(ps[:, :256], 1, c)
                self.stt(self.Sb[h], self.Sb[h], dec[1][:, c:c + 1], ps[:, :256], ALU.mult, ALU.add)
                self.PS.put(ps)
            self.small.put(dec[0], dec[1])
            Am = []
            for e in range(2):
                ps = self.PS.get()
                for c in range(nchk):
                    sl = slice(c * 128, (c + 1) * 128)
                    self.mm(ps[:, sl], ke[e][:, sl], qe[e][:, sl], True, True)
                a = self.PB.get()
                self.tt(a[:, :T], ps[:, :T], (self.maskf if e == 0 else self.maskb)[:, :T], ALU.mult)
                self.PS.put(ps)
                Am.append(a)
            o32 = []
            for vc in range(2):
                ps = self.PS.get()
                for c in range(nchk):
                    sl = slice(c * 128, (c + 1) * 128)
                    vl = self.vt_t[c // 2][:, (c % 2) * 256 + vc * 128:(c % 2) * 256 + (vc + 1) * 128]
                    self.mm(ps[:, sl], vl, Am[0][:, sl], True, False)
                    self.mm(ps[:, sl], vl, Am[1][:, sl], False, False)
                    self.mm(ps[:, sl], self.Sbf[0][c][:, vc * 128:(vc + 1) * 128], qe[0][:, sl], False, False)
                    self.mm(ps[:, sl], self.Sbf[1][c][:, vc * 128:(vc + 1) * 128], qe[1][:, sl], False, True)
                o = self.PF.get()
                self.cp(o[:, :T], ps[:, :T], e="act")
                self.PS.put(ps)
                o32.append(o)
            self.PB.put(*Am)
            self.PB.put(*qe)
            self.PB.put(*ke)
            self.free_kdvt()
            ps = self.PS.get()
            for vc in range(2):
                sq = self.PB.get()
                self.act(sq[:, :T], o32[vc][:, :T], AF.Square)
                self.mm(ps[:, :T], self.ones256b, sq[:, :T], vc == 0, vc == 1)
                self.PB.put(sq)
            rstd = self.PF.get()
            self.rsqrt_(rstd[:, :T], ps[:, :T])
            self.PS.put(ps)
            for vc in range(2):
                hc = 2 * h + vc
                self.stt(o32[vc][:, :T], o32[vc][:, :T], pv[:, P_GNG + hc:P_GNG + hc + 1], rstd[:, :T], ALU.mult, ALU.mult)
                on32.append(o32[vc])
            self.PF.put(rstd)
            if h % 2 == 1:
                half = h // 2
                slot = self.ws_next(l, B_R + half)
                for m in range(4):
                    hc = half * 4 + m
                    ps = self.PS.get()
                    self.proj_fm(ps, slot, hl, m, T)
                    sr = self.PF.get()
                    self.act(sr[:, :T], ps[:, :T], AF.Silu)
                    self.PS.put(ps)
                    o = self.PB.get()
                    self.tt(o[:, :T], on32[hc][:, :T], sr[:, :T], ALU.mult)
                    self.PF.put(sr, on32[hc])
                    on.append(o)
                self.ws_free(slot)
        self.PB.put(*adn)
        self.dbg("on", on, T)
        self.gated_out(l, B_GA, B_OGLA, on, True)
        self.dbg("ya", self.yacc, T)
        self.PB.put(*on)
        W = 256 if is_ctx else 64
        WP = W + 30
        NR = T // W
        mode = "ctx" if is_ctx else "lat"
        if self.pad_mode != mode:
            for hb in self.hcp:
                self.tk.op("dve", lambda E, hb=hb: E.memset(hb.ap, 0.0), [], [hb])
            self.pad_mode = mode
        acc = []
        accb = []
        for i in range(4):
            slot = self.ws_next(l, B_CV + i)
            for j in range(2):
                c = 2 * i + j
                ps1 = self.PS.get()
                self.proj_fm(ps1, slot, hl, j, T)
                ps2 = self.PS.get()
                self.proj_fm(ps2, slot, hl, 2 + j, T)
                sg = self.PF.get()
                self.act(sg[:, :T], ps2[:, :T], AF.Sigmoid)
                self.PS.put(ps2)
                hb = self.hcp[self.hcp_i]
                self.hcp_i ^= 1
                h3 = hb[:, :NR * WP].re("p (r w) -> p r w", w=WP)
                self.tt(h3[:, :, 15:15 + W], ps1[:, :T].re("p (r w) -> p r w", w=W), sg[:, :T].re("p (r w) -> p r w", w=W), ALU.mult)
                self.PS.put(ps1)
                self.PF.put(sg)
                dslot = self.ws_next(l, NBLK + c)
                d3 = dslot[:, :3968].re("p (j n) -> p j n", n=128)
                ps = self.PS.get()
                p3 = ps[:, :T].re("p (r w) -> p r w", w=W)
                for jj in range(31):
                    self.mm(p3, d3[:, jj, :], h3[:, :, jj:jj + W], jj == 0, jj == 30)
                self.ws_free(dslot)
                a = self.PF.get()
                self.act(a[:, :T], ps[:, :T], AF.Identity, bias=pv[:, P_CVB + c:P_CVB + c + 1])
                ab = self.PB.get()
                self.act(ab[:, :T], ps[:, :T], AF.Identity, bias=pv[:, P_CVB + c:P_CVB + c + 1])
                self.PS.put(ps)
                acc.append(a)
                accb.append(ab)
            self.ws_free(slot)
            self.pace(acc[-1], 4)
        psm = self.PS.get()
        psq = self.PS.get()
        for c in range(NCH):
            self.mm(psm[:, :T], self.onesDb, accb[c][:, :T], c == 0, c == NCH - 1)
        self.PB.put(*accb)
        for c in range(NCH):
            sq = self.PB.get()
            self.act(sq[:, :T], acc[c][:, :T], AF.Square)
            self.mm(psq[:, :T], self.onesDb, sq[:, :T], c == 0, c == NCH - 1)
            self.PB.put(sq)
        mean = self.PF.get()
        self.cp(mean[:, :T], psm[:, :T], e="act")
        self.PS.put(psm)
        var = self.PF.get()
        self.tt(var[:, :T], mean[:, :T], mean[:, :T], ALU.mult)
        self.tt(var[:, :T], psq[:, :T], var[:, :T], ALU.subtract)
        self.PS.put(psq)
        self.ts(var[:, :T], var[:, :T], 0.0, None, ALU.max)
        self.rsqrt_(var[:, :T], var[:, :T])
        cvb = []
        for c in range(NCH):
            self.tt(acc[c][:, :T], acc[c][:, :T], mean[:, :T], ALU.subtract)
            self.tt(acc[c][:, :T], acc[c][:, :T], var[:, :T], ALU.mult)
            o = self.PB.get()
            self.act(o[:, :T], acc[c][:, :T], AF.Silu, bias=pv[:, P_CLB + c:P_CLB + c + 1], scale=pv[:, P_CLG + c:P_CLG + c + 1])
            self.PF.put(acc[c])
            cvb.append(o)
        self.PF.put(mean, var)
        self.dbg("cvb", cvb, T)
        self.gated_out(l, B_GB, B_OCONV, cvb, False)
        self.dbg("yb", self.yacc, T)
        self.PB.put(*cvb)
        svg = [[None, None] for _ in range(nchk)]
        s1 = [self.small.get() for _ in range(nchk)]
        for half in range(2):
            slot = self.ws_next(l, B_SV + half)
            w3 = slot.re("p (k n) -> p k n", n=512)
            for tb in range(nchk):
                ps = self.PS.get()
                for kc in range(NCH):
                    self.mm(ps[:, :], hl[kc][:, tb * 128:(tb + 1) * 128], w3[:, kc, :], kc == 0, kc == NCH - 1)
                g = self.PF.get()
                self.act(g, ps, AF.Gelu_apprx_tanh)
                self.PS.put(ps)
                self.tk.op("dve", lambda E, o=s1[tb], half=half, g=g: E.reduce_sum(o.ap[:, half:half + 1], g.ap, AX.X), [g], [s1[tb]])
                sq = self.PF.get()
                self.act(sq, g, AF.Square)
                self.tk.op("dve", lambda E, o=s1[tb], half=half, sq=sq: E.reduce_sum(o.ap[:, 2 + half:3 + half], sq.ap, AX.X), [sq], [s1[tb]])
                self.PF.put(sq)
                svg[tb][half] = g
            self.ws_free(slot)
        svn = []
        for tb in range(nchk):
            s = s1[tb]
            self.tt(s[:, 4:5], s[:, 0:1], s[:, 1:2], ALU.add)
            self.tt(s[:, 5:6], s[:, 2:3], s[:, 3:4], ALU.add)
            self.ts(s[:, 4:6], s[:, 4:6], 1.0 / D, None, ALU.mult)
            self.tt(s[:, 6:7], s[:, 4:5], s[:, 4:5], ALU.mult)
            self.tt(s[:, 6:7], s[:, 5:6], s[:, 6:7], ALU.subtract)
            self.ts(s[:, 6:7], s[:, 6:7], 0.0, None, ALU.max)
            self.rsqrt_(s[:, 7:8], s[:, 6:7])
            self.stt(s[:, 8:9], s[:, 4:5], -1.0, s[:, 7:8], ALU.mult, ALU.mult)
            o2 = [self.PB.get(), self.PB.get()]
            for half in range(2):
                g = svg[tb][half]
                self.act(g, g, AF.Identity, bias=s[:, 8:9], scale=s[:, 7:8])
                self.tt(g, g, self.pbc[:, 1024 + half * 512:1024 + (half + 1) * 512], ALU.mult)
                self.tt(o2[half], g, self.pbc[:, 2048 + half * 512:2048 + (half + 1) * 512], ALU.add)
                self.PF.put(g)
            svn.append(o2)
            self.small.put(s)
        spo = []
        for half in range(2):
            slot = self.ws_next(l, B_SU + half)
            for m in range(4):
                g = half * 4 + m
                ps = self.PS.get()
                self.proj_fm(ps, slot, hl, m, T)
                su = self.PF.get()
                self.act(su[:, :T], ps[:, :T], AF.Gelu_apprx_tanh)
                self.PS.put(ps)
                ps = self.PS.get()
                for tb in range(nchk):
                    self.mm(ps[:, tb * 128:(tb + 1) * 128], svn[tb][half][:, m * 128:(m + 1) * 128], self.wsTb[:, g * 128:(g + 1) * 128], True, True)
                tmp = self.PF.get()
                for tb in range(nchk):
                    sl = slice(tb * 128, (tb + 1) * 128)
                    self.tt(tmp[:, sl], ps[:, sl], self.pbc[:, g * 128:(g + 1) * 128], ALU.add)
                self.PS.put(ps)
                o = self.PB.get()
                self.tt(o[:, :T], tmp[:, :T], su[:, :T], ALU.mult)
                self.PF.put(tmp, su)
                spo.append(o)
            self.ws_free(slot)
        for tb in range(nchk):
            self.PB.put(*svn[tb])
        self.dbg("spo", spo, T)
        self.gated_out(l, B_GC, B_OSGU, spo, False)
        self.dbg("yc", self.yacc, T)
        self.PB.put(*spo)
        self.PB.put(*hl)
        yb = []
        for c in range(NCH):
            o = self.PB.get()
            self.cp(o[:, :T], self.yacc[c][:, :T], e="act")
            yb.append(o)
        for half in range(2):
            slot = self.ws_next(l, B_OUT + half)
            o3 = slot.re("p (k n) -> p k n", n=512)
            for m in range(4):
                mc = half * 4 + m
                ps = self.PS.get()
                for kc in range(NCH):
                    self.mm(ps[:, :T], o3[:, kc, m * 128:(m + 1) * 128], yb[kc][:, :T], kc == 0, kc == NCH - 1)
                self.stt(self.hT[mc][:, :T], ps[:, :T], self.adaB[:, which, mc:mc + 1], self.hT[mc][:, :T], ALU.mult, ALU.add)
                self.PS.put(ps)
            self.ws_free(slot)
        self.PB.put(*yb)
        self.dbg("hmid", self.hT, T)
        if nxt is not None:
            self.prefetch_h(*nxt)
        hl2 = self.norm_mod(T, self.A2[:, which, :], self.adaB[:, which, 8:16])
        actv = []
        for j in range(11):
            slot = self.ws_next(l, B_FIN + j)
            for i in range(2):
                psg = self.PS.get()
                self.proj_fm(psg, slot, hl2, i, T)
                psu = self.PS.get()
                self.proj_fm(psu, slot, hl2, 2 + i, T)
                sg = self.PF.get()
                self.act(sg[:, :T], psg[:, :T], AF.Silu)
                self.PS.put(psg)
                o = self.PB.get()
                self.tt(o[:, :T], sg[:, :T], psu[:, :T], ALU.mult)
                self.PS.put(psu)
                self.PF.put(sg)
                actv.append(o)
            self.pace(actv[-1], 4)
            self.ws_free(slot)
        self.PB.put(*hl2)
        for mc in range(NCH):
            slot = self.ws_next(l, B_FOUT + mc)
            o3 = slot[:, :2816].re("p (k n) -> p k n", n=128)
            ps = self.PS.get()
            for kc in range(22):
                self.mm(ps[:, :T], o3[:, kc, :], actv[kc][:, :T], kc == 0, kc == 21)
            self.stt(self.hT[mc][:, :T], ps[:, :T], self.adaB[:, which, 24 + mc:25 + mc], self.hT[mc][:, :T], ALU.mult, ALU.add)
            self.PS.put(ps)
            self.ws_free(slot)
            self.pace(self.hT[mc], 4)
        self.PB.put(*actv)
        if final:
            outs = [self.PF.get() for _ in range(NCH)]
            self.norm_mod(T, pv[:, P_FG:P_FG + 8], None, out_bf=False, outs=outs)
            for c in range(NCH):
                self.dma(V(dst.buf, dst.ap[c * 128:(c + 1) * 128, t0:t0 + T]), outs[c][:, :T], "st%d" % c, q=("pool" if self.pool_dma_ok else "sp"))
            self.PF.put(*outs)
        else:
            for c in range(NCH):
                self.dma(V(dst.buf, dst.ap[c * 128:(c + 1) * 128, t0:t0 + T]), self.hT[c][:, :T], "st%d" % c, q=("pool" if self.pool_dma_ok else "sp"))

    def dbg(self, name, tiles, T):
        if not (DEBUG and self.dbg_on):
            return
        d = self.dout("dbg_" + name, [len(tiles) * 128, T])
        for c, t in enumerate(tiles):
            self.dma(V(d.buf, d.ap[c * 128:(c + 1) * 128, :]), t[:, :T], "dbg", q="pool")

    def dbg2(self, name, v, rows, ncols):
        if not DEBUG:
            return
        d = self.dout("dbg_" + name, [rows, ncols])
        self.dma(d, v, "dbg", q="pool")

    def stage_p2(self, l):
        self.flush_conv(l)
        self.setup_late(l)
        if l == 0:
            self.p2_tile(l, self.csrc[0], self.cdst[0], 0, CTX, 1, True, None, False, nxt=(self.hsrc[l], 0, TL))
        self.pool_dma_ok = (l != self.stages[0][1])
        self.stage_x(l)
        items = self.late_plan(l)
        nsl = max(1, self.NT - 1)
        per = (len(items) + nsl - 1) // nsl
        for n in range(self.NT):
            self.p2_tile(l, self.hsrc[l], self.hdst[l], n * TL, TL, 0, False, n, l == 1, nxt=((self.hsrc[l], (n + 1) * TL, TL) if n + 1 < self.NT else None))
            for it in items[:per]:
                it()
            items = items[per:]
        for it in items:
            it()


def _blk(W, cols):
    K = W.shape[0]
    sub = W[:, cols]
    kc = K // 128
    a = sub.reshape(kc, 128, sub.shape[1]).transpose(1, 0, 2).reshape(128, -1)
    out = np.zeros((128, 4096), np.float32)
    out[:, :a.shape[1]] = a
    return out


def _fm(v):
    return np.ascontiguousarray(v.reshape(-1, 128).T)


def prep_layer(inp, l):
    w_in = inp["w_in"][l]
    r = np.arange
    blks = []
    for h in range(4):
        cols = np.concatenate([Q0 + h * 128 + r(128), K0 + h * 128 + r(128), V0 + h * 256 + r(256)])
        blks.append(_blk(w_in, cols))
    for i in range(2):
        blks.append(_blk(w_in, R0 + i * 512 + r(512)))
    for i in range(4):
        cols = np.concatenate([C10 + i * 256 + r(256), C20 + i * 256 + r(256)])
        blks.append(_blk(w_in, cols))
    for i in range(2):
        blks.append(_blk(w_in, SU0 + i * 512 + r(512)))
    for i in range(2):
        blks.append(_blk(w_in, SV0 + i * 512 + r(512)))
    for i in range(6):
        blks.append(_blk(w_in, GT0 + i * 512 + r(512)))
    for name in ("w_o_gla", "w_o_conv", "w_o_sgu", "w_out"):
        for i in range(2):
            blks.append(_blk(inp[name][l], i * 512 + r(512)))
    wf = inp["w_ffn_in"][l]
    for j in range(11):
        cols = np.concatenate([j * 256 + r(256), DFF + j * 256 + r(256)])
        blks.append(_blk(wf, cols))
    wo = inp["w_ffn_out"][l]
    for mc in range(8):
        blks.append(_blk(wo, mc * 128 + r(128)))
    assert len(blks) == NBLK
    d = {}
    d["wblk%d" % l] = np.stack(blks)
    d["wadn%d" % l] = np.ascontiguousarray(_blk(w_in, A0 + r(32))[:, :256])
    d["wada%d" % l] = np.stack([_blk(inp["w_ada"][l], j * 512 + r(512)) for j in range(12)])
    pv = np.zeros((128, NPV), np.float32)
    pv[:, P_N1G:P_N1G + 8] = _fm(inp["norm1_g"][l])
    pv[:, P_N2G:P_N2G + 8] = _fm(inp["norm2_g"][l])
    pv[:, P_GNG:P_GNG + 8] = _fm(inp["gla_norm_g"][l])
    pv[:, P_CVB:P_CVB + 8] = _fm(inp["conv_b"][l])
    pv[:, P_CLG:P_CLG + 8] = _fm(inp["conv_ln_g"][l])
    pv[:, P_CLB:P_CLB + 8] = _fm(inp["conv_ln_b"][l])
    pv[:, P_BADA:P_BADA + 48] = _fm(inp["b_ada"][l])
    cw = inp["conv_w"][l]
    pv[:, P_CW:P_CW + 248] = cw.T.reshape(8, 128, 31).transpose(1, 0, 2).reshape(128, 248)
    pv[:, P_AB:P_AB + 8] = _fm(inp["gla_a_b"][l].reshape(-1))
    pv[:, P_FG:P_FG + 8] = _fm(inp["final_g"])
    d["pvec%d" % l] = pv
    d["aup%d" % l] = np.ascontiguousarray(inp["gla_a_up"][l].transpose(1, 0, 2).reshape(16, 1024))
    d["wsT%d" % l] = np.ascontiguousarray(inp["sgu_ws"][l].transpose(2, 0, 1).reshape(128, 1024))
    pbc = np.concatenate([inp["sgu_b"][l].reshape(-1), inp["sgu_ln_g"][l], inp["sgu_ln_b"][l]])
    d["pbc%d" % l] = np.ascontiguousarray(np.broadcast_to(pbc[None, :], (128, 3072)))
    return d


_CACHE = {}


def _get(NT, stages, fused):
    key = (NT, tuple(stages), fused)
    if key not in _CACHE:
        _CACHE[key] = Builder(NT, list(stages), fused)
    return _CACHE[key]


def _core_common(inp, core):
    b, j = core // 4, core % 4
    cv = np.zeros((128, 8, 2), np.float32)
    cv[:, :, 0] = _fm(inp["c"][b])
    cv[:, :, 1] = _fm(inp["c_ctx"])
    fl = np.zeros((128, 8), np.float32)
    for i in range(4):
        fl[:, i] = 1.0 if i < j else 0.0
        fl[:, 4 + i] = 1.0 if i > j else 0.0
    return {"cvec": cv.reshape(128, 16), "flags": fl}


FUSED = True
REUSE = True
SCHED_WINDOW = 3000
SCHED_QUANT = 300.0
DEBUG = False


def kernel(**inp):
    inp = {k: np.asarray(v, np.float32) for k, v in inp.items()}
    x = inp["x"]
    B, S, _ = x.shape
    slab = S // 4
    NT = slab // TL
    lay = [prep_layer(inp, l) for l in range(2)]
    xT = [np.ascontiguousarray(x[c // 4, (c % 4) * slab:(c % 4 + 1) * slab, :].T) for c in range(8)]
    cT = [np.ascontiguousarray(inp["ctx"][b].T) for b in range(2)]
    cores = list(range(8))
    com = [_core_common(inp, c) for c in cores]
    if FUSED:
        bd = _get(NT, [("P1", 0), ("P2", 0), ("P1", 1), ("P2", 1)], True)
        maps = []
        for c in cores:
            m = dict(com[c])
            m.update(lay[0])
            m.update(lay[1])
            m["xT"] = xT[c]
            m["ctxT"] = cT[c // 4]
            maps.append(m)
        res = run_bass_kernel_spmd(bd.nc, maps, core_ids=cores).results
        outT = [res[c]["outT"] for c in cores]
    else:
        bdA = _get(NT, [("P1", 0)], False)
        maps = []
        for c in cores:
            m = dict(com[c]); m.update(lay[0]); m["xT"] = xT[c]; m["ctxT"] = cT[c // 4]
            maps.append(m)
        rA = run_bass_kernel_spmd(bdA.nc, maps, core_ids=cores).results
        bdB = _get(NT, [("P2", 0), ("P1", 1)], False)
        maps = []
        for c in cores:
            m = dict(com[c]); m.update(lay[0]); m.update(lay[1]); m["xT"] = xT[c]; m["ctxT"] = cT[c // 4]
            b = c // 4
            m["st_in"] = np.concatenate([rA[b * 4 + i]["st_out"] for i in range(4)], axis=0)
            m["sbs_in"] = rA[c]["sbs_out"]
            maps.append(m)
        rB = run_bass_kernel_spmd(bdB.nc, maps, core_ids=cores).results
        bdC = _get(NT, [("P2", 1)], False)
        maps = []
        for c in cores:
            m = dict(com[c]); m.update(lay[1]); m["h1"] = rB[c]["h1"]
            b = c // 4
            m["st_in"] = np.concatenate([rB[b * 4 + i]["st_out"] for i in range(4)], axis=0)
            m["sbs_in"] = rB[c]["sbs_out"]
            maps.append(m)
        rC = run_bass_kernel_spmd(bdC.nc, maps, core_ids=cores).results
        outT = [rC[c]["outT"] for c in cores]
    out = np.empty((B, S, D), np.float32)
    for c in cores:
        out[c // 4, (c % 4) * slab:(c % 4 + 1) * slab, :] = outT[c].T
    return out
```

```python
import numpy as np
import concourse.bass as bass
import concourse.mybir as mybir
from concourse.bass_utils import run_bass_kernel_spmd

F32 = mybir.dt.float32
BF16 = mybir.dt.bfloat16
AF = mybir.ActivationFunctionType
ALU = mybir.AluOpType
AX = mybir.AxisListType

D = 1024
NCH = 8
DFF = 2816
NBLK = 47
EPS = 1e-6
CTX = 256
TL = 512
NREC = 4104


class Buf:
    __slots__ = ("name", "w", "r")

    def __init__(self, name):
        self.name = name
        self.w = None
        self.r = {}


class V:
    __slots__ = ("buf", "ap")

    def __init__(self, buf, ap):
        self.buf = buf
        self.ap = ap

    def __getitem__(self, idx):
        return V(self.buf, self.ap[idx])

    def re(self, s, **kw):
        return V(self.buf, self.ap.rearrange(s, **kw))


class Pool:
    def __init__(self, name, vs):
        self.name = name
        self.free = list(vs)

    def get(self):
        if not self.free:
            raise RuntimeError("pool empty " + self.name)
        return self.free.pop(0)

    def put(self, *vs):
        for v in vs:
            self.free.append(v)


class TK:
    def __init__(self, nc):
        self.nc = nc
        self.E = {"pe": nc.tensor, "dve": nc.vector, "act": nc.scalar, "pool": nc.gpsimd, "sp": nc.sync}
        self.sem = {}
        self.cnt = {}
        self.seen = {e: {} for e in self.E}
        for e in ("pe", "dve", "act", "pool"):
            self.sem[e] = nc.alloc_semaphore("c_" + e)
            self.cnt[e] = 0
        self.nins = 0
        self.pending = []

    def _sem(self, key):
        if key not in self.sem:
            self.sem[key] = self.nc.alloc_semaphore("d_" + key)
            self.cnt[key] = 0
        return self.sem[key]

    def _deps(self, reads, writes):
        d = {}

        def add(k, val):
            if d.get(k, 0) < val:
                d[k] = val

        for v in reads:
            if v.buf.w:
                add(*v.buf.w)
        for v in writes:
            if v.buf.w:
                add(*v.buf.w)
            for k, val in v.buf.r.items():
                add(k, val)
        return d

    def _wait(self, e, deps):
        for key, val in deps.items():
            if e == "pe" and key == "pe":
                continue
            if self.seen[e].get(key, 0) < val:
                self.E[e].wait_ge(self.sem[key], val)
                self.seen[e][key] = val

    def op(self, e, fn, reads, writes, tbl=None):
        w = writes[0].ap
        n = 1
        for d in w.shape[1:]:
            n *= d
        if e == "pe":
            dur = max(n, 64) / 1.95 + 40
            if reads and reads[0].ap.dtype == F32:
                dur *= 4
        elif e == "dve":
            dur = 0.95 * n + 60
        elif e == "act":
            dur = 1.0 * n + 120
        else:
            dur = 2.0 * n + 100
        self.pending.append(("op", e, fn, list(reads), list(writes), dur, dur, tbl))

    def dma(self, q, out, in_, key, pace=()):
        nb = 1
        for d in out.ap.shape:
            nb *= d
        nb *= 2 if out.ap.dtype == BF16 else 4
        busy = 60.0
        if key.startswith("cv"):
            busy = 10000.0
        self.pending.append(("dma", q, (out, in_, key, list(pace)), [in_], [out], busy, 2000.0 + busy + nb / 150.0, None, list(pace)))

    def custom(self, e, fn, reads, writes, dur):
        self.pending.append(("custom", e, fn, list(reads), list(writes), dur, dur, None))

    def flush(self, window=700):
        P = self.pending
        self.pending = []
        n = len(P)
        if n == 0:
            return
        preds = [None] * n
        lastw, readers = {}, {}
        for i, rec in enumerate(P):
            ps = set()
            for v in rec[3]:
                b = id(v.buf)
                if b in lastw:
                    ps.add(lastw[b])
            for v in rec[4]:
                b = id(v.buf)
                if b in lastw:
                    ps.add(lastw[b])
                for r in readers.get(b, ()):
                    ps.add(r)
            if len(rec) > 8:
                for v in rec[8]:
                    b = id(v.buf)
                    if b in lastw:
                        ps.add(lastw[b])
            ps.discard(i)
            preds[i] = ps
            for v in rec[3]:
                readers.setdefault(id(v.buf), []).append(i)
            for v in rec[4]:
                b = id(v.buf)
                lastw[b] = i
                readers[b] = []
        succs = [[] for _ in range(n)]
        indeg = [0] * n
        for i in range(n):
            indeg[i] = len(preds[i])
            for p in preds[i]:
                succs[p].append(i)
        bl = [0.0] * n
        for i in range(n - 1, -1, -1):
            m = 0.0
            for sc in succs[i]:
                if bl[sc] > m:
                    m = bl[sc]
            bl[i] = P[i][6] + m
        engs = ["pe", "dve", "act", "pool", "sp"]
        etime = {e: 0.0 for e in engs}
        etbl = {e: None for e in engs}
        rdy = {e: [] for e in engs}
        rt = [0.0] * n
        fin = [0.0] * n
        done = [False] * n
        for i in range(n):
            if indeg[i] == 0:
                rdy[P[i][1]].append(i)
        lo = 0
        order = []
        nsched = 0
        while nsched < n:
            while lo < n and done[lo]:
                lo += 1
            lim = lo + window
            best = None
            for e in engs:
                lst = rdy[e]
                if not lst:
                    continue
                et = etime[e]
                tb = etbl[e]
                for i in lst:
                    if i >= lim:
                        continue
                    st = rt[i] if rt[i] > et else et
                    t = P[i][7]
                    if t is not None and tb is not None and t != tb:
                        st += 1300.0
                    key = (int((st + 0.02 * (i - lo)) / SCHED_QUANT), -bl[i], i)
                    if best is None or key < best[0]:
                        best = (key, i, e, st)
            if best is None:
                cand = [(min(l), e) for e, l in rdy.items() if l]
                i, e = min(cand)
                st = max(rt[i], etime[e])
            else:
                _, i, e, st = best
            rec = P[i]
            rdy[e].remove(i)
            etime[e] = st + rec[5]
            if rec[7] is not None:
                etbl[e] = rec[7]
            fin[i] = st + rec[6]
            done[i] = True
            nsched += 1
            order.append(i)
            for sc in succs[i]:
                lat = (SCHED_SELF_LAT if P[sc][1] == e and rec[0] == "op" else SCHED_XLAT)
                if fin[i] + lat > rt[sc]:
                    rt[sc] = fin[i] + lat
                indeg[sc] -= 1
                if indeg[sc] == 0:
                    rdy[P[sc][1]].append(sc)
        self.sched_est = max(etime.values())
        for i in order:
            rec = P[i]
            if rec[0] == "op":
                self._emit_op(rec[1], rec[2], rec[3], rec[4])
            elif rec[0] == "dma":
                self._emit_dma(rec[1], *rec[2])
            else:
                rec[2](self)

    def _emit_op(self, e, fn, reads, writes):
        self._wait(e, self._deps(reads, writes))
        ins = fn(self.E[e])
        self.cnt[e] += 1
        ins.then_inc(self.sem[e], 1)
        self.nins += 1
        c = self.cnt[e]
        for v in reads:
            v.buf.r[e] = c
        for v in writes:
            v.buf.w = (e, c)
            v.buf.r = {}
        return ins

    def _emit_dma(self, q, out, in_, key, pace=()):
        self._wait(q, self._deps([in_] + list(pace), [out]))
        sem = self._sem(key)
        if self.cnt[key] > 0 and self.seen[q].get(key, 0) < self.cnt[key]:
            self.E[q].wait_ge(sem, self.cnt[key])
            self.seen[q][key] = self.cnt[key]
        ins = self.E[q].dma_start(out=out.ap, in_=in_.ap)
        self.cnt[key] += 16
        ins.then_inc(sem, 16)
        self.nins += 1
        c = self.cnt[key]
        in_.buf.r[key] = c
        out.buf.w = (key, c)
        out.buf.r = {}

    def wait_all(self, e, bufs):
        d = {}
        for b in bufs:
            if b.w and d.get(b.w[0], 0) < b.w[1]:
                d[b.w[0]] = b.w[1]
        for key, val in d.items():
            self.E[e].wait_ge(self.sem[key], val)


Q0, K0, V0, A0, R0, C10, C20, SU0, SV0, GT0 = 0, 512, 1024, 2048, 2080, 3104, 4128, 5152, 6176, 7200
B_QKV, B_R, B_CV, B_SU, B_SV, B_GA, B_GB, B_GC = 0, 4, 6, 10, 12, 14, 16, 18
B_OGLA, B_OCONV, B_OSGU, B_OUT, B_FIN, B_FOUT = 20, 22, 24, 26, 28, 39
P_N1G, P_N2G, P_GNG, P_CVB, P_CLG, P_CLB, P_BADA, P_CW, P_AB, P_FG, NPV = 0, 8, 16, 24, 32, 40, 48, 96, 344, 352, 360


P2_ORDER = ([0, 1, 4, 2, 3, 5, 14, 20, 15, 21] + [6, 47, 48, 7, 49, 50, 8, 51, 52, 9, 53, 54, 16, 22, 17, 23] + [12, 13, 10, 11, 18, 24, 19, 25]
            + [26, 27] + list(range(28, 39)) + list(range(39, 47)))
assert sorted(P2_ORDER) == list(range(NBLK + 8))


class Builder:
    def __init__(self, NT, stages, fused):
        self.NT = NT
        self.TOK = NT * TL
        self.stages = stages
        self.fused = fused
        nc = self.nc = bass.Bass("TRN2", target_bir_lowering=False)
        self.tk = TK(nc)
        self.dram = {}
        self.build()

    def din(self, name, shape, dt=F32):
        t = self.nc.dram_tensor(name, list(shape), dt, kind="ExternalInput")
        v = V(Buf(name), t.ap())
        self.dram[name] = v
        return v

    def dout(self, name, shape, dt=F32):
        t = self.nc.dram_tensor(name, list(shape), dt, kind="ExternalOutput")
        v = V(Buf(name), t.ap())
        self.dram[name] = v
        return v

    def dint(self, name, shape, dt=F32):
        t = self.nc.dram_tensor(name, list(shape), dt, kind="Internal")
        return V(Buf(name), t.ap())

    def sb(self, name, shape, dt=F32):
        return V(Buf(name), self.nc.alloc_sbuf_tensor(name, list(shape), dt).ap())

    def mm(self, ps, lhsT, rhs, start, stop):
        self.tk.op("pe", lambda E: E.matmul(ps.ap, lhsT=lhsT.ap, rhs=rhs.ap, start=start, stop=stop), [lhsT, rhs], [ps])

    def tr(self, ps, in_):
        self.tk.op("pe", lambda E: E.transpose(ps.ap, in_.ap, self.ident.ap), [in_, self.ident], [ps])

    def act(self, out, in_, func, bias=None, scale=None):
        reads = [in_]
        kw = {}
        if bias is not None:
            if isinstance(bias, V):
                reads.append(bias)
                kw["bias"] = bias.ap
            else:
                kw["bias"] = float(bias)
        if scale is not None:
            if isinstance(scale, V):
                reads.append(scale)
                kw["scale"] = scale.ap
            else:
                kw["scale"] = float(scale)
        tbl = "A" if func in (AF.Exp, AF.Ln) else ("B" if func in (AF.Sigmoid, AF.Silu, AF.Gelu_apprx_tanh) else None)
        self.tk.op("act", lambda E: E.activation(out.ap, in_.ap, func, **kw), reads, [out], tbl=tbl)

    def cbias(self, val):
        return self.constv.ap[:, self.cidx[val]:self.cidx[val] + 1]

    def tt(self, out, a, b, op, e="dve"):
        self.tk.op(e, lambda E: E.tensor_tensor(out.ap, a.ap, b.ap, op), [a, b], [out])

    def ts(self, out, in0, s1, s2, op0, op1=None, e="dve"):
        reads = [in0]
        a1 = s1.ap if isinstance(s1, V) else float(s1)
        if isinstance(s1, V):
            reads.append(s1)
        a2 = None
        if s2 is not None:
            a2 = s2.ap if isinstance(s2, V) else float(s2)
            if isinstance(s2, V):
                reads.append(s2)
        if op1 is None:
            self.tk.op(e, lambda E: E.tensor_scalar(out.ap, in0.ap, a1, None, op0), reads, [out])
        else:
            self.tk.op(e, lambda E: E.tensor_scalar(out.ap, in0.ap, a1, a2, op0, op1), reads, [out])

    def stt(self, out, in0, scalar, in1, op0, op1):
        reads = [in0, in1]
        a = scalar.ap if isinstance(scalar, V) else float(scalar)
        if isinstance(scalar, V):
            reads.append(scalar)
        self.tk.op("dve", lambda E: E.scalar_tensor_tensor(out.ap, in0.ap, a, in1.ap, op0, op1), reads, [out])

    def cp(self, out, in_, e="dve"):
        if e == "act":
            self.tk.op("act", lambda E: E.copy(out.ap, in_.ap), [in_], [out])
        else:
            self.tk.op(e, lambda E: E.tensor_copy(out.ap, in_.ap), [in_], [out])

    def dma(self, out, in_, key, q="sp", pace=()):
        self.tk.dma(q, out, in_, key, pace)

    def pace(self, v, stride=1, burst=1):
        self.pace_cnt += 1
        if self.pace_cnt % stride:
            return
        for _ in range(burst):
            if not self.conv_pending:
                return
            l, b = self.conv_pending.pop(0)
            self.emit_conv(l, [b], pace=[v])

    def flush_conv(self, l):
        rest = [x for x in self.conv_pending if x[0] == l]
        self.conv_pending = [x for x in self.conv_pending if x[0] != l]
        for (l2, b) in rest:
            self.emit_conv(l2, [b])

    def rsqrt_(self, out, in_, eps=EPS):
        self.act(out, in_, AF.Ln, bias=eps)
        self.act(out, out, AF.Exp, scale=-0.5)

    def ws_init(self, seq):
        self.wseq = seq
        self.wpos = 0
        self.wiss = 0
        self.wpending = []

    def ws_issue(self):
        l, b = self.wseq[self.wiss]
        slot = self.wslots.get()
        if b >= NBLK:
            self.dma(slot[:, :3968], V(self.wbf[l][b].buf, self.wbf[l][b].ap[:, :3968]), "w%d" % (self.wiss % 4))
        else:
            self.dma(slot, self.wbf[l][b], "w%d" % (self.wiss % 4))
        self.wpending.append(slot)
        self.wiss += 1

    def ws_next(self, l, b, depth=2):
        assert self.wseq[self.wpos] == (l, b), (self.wpos, self.wseq[self.wpos], (l, b))
        while self.wiss < len(self.wseq) and self.wiss <= self.wpos + depth and (self.wslots.free or self.wiss <= self.wpos):
            self.ws_issue()
        slot = self.wpending.pop(0)
        self.wpos += 1
        return slot

    def ws_free(self, slot):
        self.wslots.put(slot)

    def build(self):
        nc, tk = self.nc, self.tk
        NT, TOK = self.NT, self.TOK
        stages = self.stages
        layers = sorted({l for (_, l) in stages})
        self.flags_d = self.din("flags", [128, 8])
        self.cvec_d = self.din("cvec", [128, 16])
        self.wblk, self.wadn, self.wada, self.pvec_d, self.aup_d, self.wsT_d, self.pbc_d = {}, {}, {}, {}, {}, {}, {}
        self.wbf = {}
        for l in layers:
            self.wblk[l] = self.din("wblk%d" % l, [NBLK, 128, 4096])
            self.wadn[l] = self.din("wadn%d" % l, [128, 256])
            self.wada[l] = self.din("wada%d" % l, [12, 128, 4096])
            self.pvec_d[l] = self.din("pvec%d" % l, [128, NPV])
            self.aup_d[l] = self.din("aup%d" % l, [16, 1024])
            self.wsT_d[l] = self.din("wsT%d" % l, [128, 1024])
            self.pbc_d[l] = self.din("pbc%d" % l, [128, 3072])
            t = nc.dram_tensor("wbf%d" % l, [NBLK + 8, 128, 4096], BF16, kind="Internal").ap()
            self.wbf[l] = [V(Buf("wbf%d_%d" % (l, b)), t[b]) for b in range(NBLK + 8)]
        first, last = stages[0], stages[-1]
        self.hsrc, self.hdst, self.csrc, self.cdst = {}, {}, {}, {}
        if ("P1", 0) in stages or ("P2", 0) in stages:
            self.hsrc[0] = self.din("xT", [D, TOK])
            self.csrc[0] = self.din("ctxT", [D, CTX])
        if ("P2", 0) in stages:
            if ("P2", 1) in stages:
                h1 = self.dint("h1", [D, TOK])
                c1 = self.dint("hctx1", [D, CTX])
            else:
                h1 = self.dout("h1", [D, TOK])
                c1 = self.dout("hctx1", [D, CTX])
            self.hdst[0] = h1
            self.cdst[0] = c1
            self.hsrc[1] = h1
            self.csrc[1] = c1
        elif ("P1", 1) in stages or ("P2", 1) in stages:
            self.hsrc[1] = self.din("h1", [D, TOK])
            if ("P1", 1) in stages:
                self.csrc[1] = self.din("hctx1", [D, CTX])
        if ("P2", 1) in stages:
            self.hdst[1] = self.dout("outT", [D, TOK])
        self.rec_mine, self.rec_all, self.sbs = {}, {}, {}
        PW = [1024, 1024, 1024, 1024, 128]
        for l in layers:
            if ("P1", l) in stages:
                if self.fused:
                    self.rec_mine[l] = [self.dint("rec%d_%d" % (l, p), [128, PW[p]]) for p in range(5)]
                else:
                    t = self.dout("st_out", [128, NREC])
                    self.rec_mine[l] = [V(Buf("st_out%d" % p), t.ap[:, p * 1024:min(NREC, (p + 1) * 1024)]) for p in range(5)]
            if ("P2", l) in stages:
                if self.fused:
                    self.rec_all[l] = [self.dint("recall%d_%d" % (l, p), [4 * 128, PW[p]]) for p in range(5)]
                else:
                    t = self.din("st_in", [4 * 128, NREC])
                    self.rec_all[l] = [V(Buf("st_in%d" % p), t.ap[:, p * 1024:min(NREC, (p + 1) * 1024)]) for p in range(5)]
                self.sbs[l] = None
        self.gsc, self.gdec = {}, {}
        for l in layers:
            if REUSE and self.fused and ("P1", l) in stages and ("P2", l) in stages:
                ne = (NT + 1) * 4
                t = nc.dram_tensor("gsc%d" % l, [ne, 128, 3072], BF16, kind="Internal").ap()
                self.gsc[l] = [V(Buf("gsc%d_%d" % (l, i)), t[i]) for i in range(ne)]
                t = nc.dram_tensor("gdec%d" % l, [ne, 128, 8], F32, kind="Internal").ap()
                self.gdec[l] = [V(Buf("gdec%d_%d" % (l, i)), t[i]) for i in range(ne)]
        for l in layers:
            if ("P2", l) in stages:
                if ("P1", l) in stages:
                    t = nc.dram_tensor("sbs%d" % l, [NT, 128, 1028], F32, kind="Internal").ap()
                    self.sbs[l] = [V(Buf("sbs%d_%d" % (l, n)), t[n]) for n in range(NT)]
                else:
                    t = self.din("sbs_in", [NT * 128, 1028])
                    self.sbs[l] = [V(Buf("sbs_in%d" % n), t.ap[n * 128:(n + 1) * 128, :]) for n in range(NT)]
            elif ("P1", l) in stages:
                t = self.dout("sbs_out", [NT * 128, 1028])
                self.sbs[l] = [V(Buf("sbs_out%d" % n), t.ap[n * 128:(n + 1) * 128, :]) for n in range(NT)]

        sb = self.sb
        self.ident = sb("ident", [128, 128], BF16)
        self.identf = sb("identf", [128, 128])
        self.hcp = [sb("hcp%d" % i, [128, 752], BF16) for i in range(2)]
        self.pad_mode = None
        self.hcp_i = 0
        self.onesf = sb("onesf", [128, 128])
        self.onesD = sb("onesD", [128, 128])
        self.maskf = sb("maskf", [128, 512], BF16)
        self.maskb = sb("maskb", [128, 512], BF16)
        self.constv = sb("constv", [128, 4])
        self.cidx = {EPS: 0, 1.0: 1, float(np.log(128.0 ** -0.5)): 2, 0.0: 3}
        self.LNSC = float(np.log(128.0 ** -0.5))
        self.flags = sb("flags_s", [128, 8])
        self.cvec = sb("cvec_s", [128, 16])
        self.LP = {}
        for l in layers:
            self.LP[l] = dict(pvec=sb("pvec_s%d" % l, [128, NPV]), negab=sb("negab%d" % l, [128, 8]), adaA=sb("adaA%d" % l, [128, 2, 16]), adaB=sb("adaB%d" % l, [128, 2, 32]),
                              A1=sb("A1_%d" % l, [128, 2, 8]), A2=sb("A2_%d" % l, [128, 2, 8]), aupb=sb("aupb%d" % l, [16, 1024], BF16),
                              adnw=sb("adnw%d" % l, [128, 256], BF16))
        self.wsTb = sb("wsTb", [128, 1024], BF16)
        self.pbc = sb("pbc", [128, 3072])
        self.onesDb = sb("onesDb", [128, 128], BF16)
        self.ones256b = sb("ones256b", [128, 128], BF16)
        self.Sf = [sb("Sf%d" % h, [128, 256]) for h in range(4)]
        self.Sb = [sb("Sb%d" % h, [128, 256]) for h in range(4)]
        self.Sbin = [sb("Sbin%d" % h, [128, 256]) for h in range(4)]
        self.Sacc = [sb("Sacc%d" % h, [128, 256]) for h in range(4)]
        self.Tf = sb("Tf", [128, 256])
        self.Dsuf = sb("Dsuf", [128, 4])
        self.Abc = sb("Abc", [128, 4])
        self.small = Pool("small", [sb("sm%d" % i, [128, 16]) for i in range(12)])
        self.wslots = Pool("wslots", [sb("wslot%d" % i, [128, 4096], BF16) for i in range(4)])
        self.hTs = [[sb("hT%d_%d" % (k, c), [128, TL]) for c in range(NCH)] for k in range(2)]
        self.hT = self.hTs[0]
        self.h_loaded = None
        self.yacc = [sb("yacc%d" % c, [128, TL]) for c in range(NCH)]
        self.PF = Pool("PF", [sb("pf%d" % i, [128, 512]) for i in range(19)])
        self.PB = Pool("PB", [sb("pb%d" % i, [128, 512], BF16) for i in range(32)])
        self.Sbf = [[sb("Sbf%d_%d" % (e, c), [128, 256], BF16) for c in range(4)] for e in range(2)]
        self.PS = Pool("PS", [V(Buf("ps%d" % i), nc.alloc_psum_tensor("ps%d" % i, [128, 512], F32).ap()) for i in range(7)])
        self.pst = V(Buf("pst"), nc.alloc_psum_tensor("pst", [128, 1024], BF16).ap())

        P = "pool"
        tk.op(P, lambda E: E.memset(self.ident.ap, 1.0), [], [self.ident])
        tk.op(P, lambda E: E.affine_select(out=self.ident.ap, in_=self.ident.ap, pattern=[[-1, 128]], compare_op=ALU.is_equal, fill=0.0, base=0, channel_multiplier=1), [self.ident], [self.ident])
        tk.op(P, lambda E: E.memset(self.identf.ap, 1.0), [], [self.identf])
        tk.op(P, lambda E: E.affine_select(out=self.identf.ap, in_=self.identf.ap, pattern=[[-1, 128]], compare_op=ALU.is_equal, fill=0.0, base=0, channel_multiplier=1), [self.identf], [self.identf])
        tk.op(P, lambda E: E.memset(self.onesf.ap, 1.0), [], [self.onesf])
        tk.op(P, lambda E: E.memset(self.onesD.ap, 1.0 / D), [], [self.onesD])
        tk.op(P, lambda E: E.memset(self.onesDb.ap, 1.0 / D), [], [self.onesDb])
        tk.op(P, lambda E: E.memset(self.ones256b.ap, 1.0 / 256), [], [self.ones256b])
        tk.op(P, lambda E: E.memset(self.maskf.ap, 1.0), [], [self.maskf])
        tk.op(P, lambda E: E.memset(self.maskb.ap, 1.0), [], [self.maskb])
        for c in range(4):
            mf = self.maskf[:, c * 128:(c + 1) * 128]
            mb = self.maskb[:, c * 128:(c + 1) * 128]
            tk.op(P, lambda E, mf=mf: E.affine_select(out=mf.ap, in_=mf.ap, pattern=[[1, 128]], compare_op=ALU.is_ge, fill=0.0, base=0, channel_multiplier=-1), [mf], [mf])
            tk.op(P, lambda E, mb=mb: E.affine_select(out=mb.ap, in_=mb.ap, pattern=[[-1, 128]], compare_op=ALU.is_gt, fill=0.0, base=0, channel_multiplier=1), [mb], [mb])
        for val, i in self.cidx.items():
            cv = self.constv[:, i:i + 1]
            tk.op(P, lambda E, cv=cv, val=val: E.memset(cv.ap, val), [], [cv])
        self.dma(self.flags, self.flags_d, "misc")
        self.dma(self.cvec, self.cvec_d, "misc")
        self.scT = sb("scT", [128, 16])
        self.act(self.scT, self.cvec, AF.Silu)

        for l in layers:
            self.dma(self.LP[l]["adnw"], self.wadn[l], "adn%d" % l, q="pool")
        need = {}
        for (st, l) in stages:
            blks = [0, 1, 2, 3] if st == "P1" else [0, 1, 2, 3] + [b for b in P2_ORDER if b >= 4 and b < NBLK]
            cur = need.setdefault(l, [])
            for b in blks:
                if b not in cur:
                    cur.append(b)
        self.need = need
        self.conv_done = set()
        l0 = stages[0][1]
        self.emit_conv(l0, [0, 1, 2, 3])
        self.pace_cnt = 0
        self.conv_pending = [(l, b) for l in layers for b in need[l] if (l, b) not in self.conv_done]
        self.pool_dma_ok = False

        seq = []
        for (st, l) in stages:
            if st == "P1":
                seq += [(l, b) for _ in range(NT + 1) for b in range(4)]
            else:
                ntile = NT + (1 if l == 0 else 0)
                seq += [(l, b) for _ in range(ntile) for b in P2_ORDER]
        self.ws_init(seq)

        self.layer_setup(l0)
        self.setup_done = {l0}
        self.hsel = 0
        for (st, l) in stages:
            self.use_layer(l)
            if st == "P1":
                self.stage_p1(l)
            else:
                self.stage_p2(l)
        import time as _t
        _t0 = _t.time()
        tk.flush(SCHED_WINDOW)
        print("[kernel] scheduled %d instrs in %.1fs, est %.2f ms" % (tk.nins, _t.time() - _t0, getattr(tk, "sched_est", 0) / 1e6), flush=True)
        outs = [v.buf for k, v in self.dram.items() if k in ("outT", "h1", "hctx1", "st_out", "sbs_out")]
        extra = []
        for l in self.sbs:
            if self.sbs[l] and not self.fused:
                extra += [v.buf for v in self.sbs[l]]
        tk.wait_all("sp", outs + extra)

    def emit_conv(self, l, blks=None, pace=()):
        for b in (self.need[l] if blks is None else blks):
            if (l, b) in self.conv_done:
                continue
            self.conv_done.add((l, b))
            src = V(self.wblk[l].buf, self.wblk[l].ap[b].rearrange("p (a n) -> (p a) n", n=2048))
            dst = V(self.wbf[l][b].buf, self.wbf[l][b].ap.rearrange("p (a n) -> (p a) n", n=2048))
            self.dma(dst, src, "cv%d" % (b % 4), q="pool", pace=pace)

    def late_plan(self, cur):
        items = []
        for l in sorted(self.need):
            if l in self.setup_done:
                continue
            self.setup_done.add(l)
            items.append(lambda l=l: (self.layer_setup(l, ada=False), self.use_layer(cur)))
            for j in range(12):
                items.append(lambda l=l, j=j: self.ada_piece(l, j))
            items.append(lambda l=l: self.ada_finish(l))
        return items

    def use_layer(self, l):
        for k, v in self.LP[l].items():
            setattr(self, k, v)

    def setup_late(self, l):
        wtmp = [self.PF.get(), self.PF.get()]
        for i in range(2):
            self.dma(wtmp[i], V(self.wsT_d[l].buf, self.wsT_d[l].ap[:, i * 512:(i + 1) * 512]), "misc%d" % i)
            self.cp(self.wsTb[:, i * 512:(i + 1) * 512], wtmp[i])
        self.PF.put(*wtmp)
        self.dma(self.pbc, self.pbc_d[l], "misc2")

    def layer_setup(self, l, ada=True):
        self.use_layer(l)
        pv = self.pvec
        self.dma(self.pvec, self.pvec_d[l], "misc0")
        wtmp = [self.PF.get(), self.PF.get()]
        for i in range(2):
            self.dma(wtmp[i][:16, :], V(self.aup_d[l].buf, self.aup_d[l].ap[:, i * 512:(i + 1) * 512]), "misc%d" % (1 + i))
            self.cp(self.aupb[:, i * 512:(i + 1) * 512], wtmp[i][:16, :])
        self.PF.put(*wtmp)
        self.ts(self.negab, pv[:, P_AB:P_AB + 8], -1.0, None, ALU.mult)
        if ("P2", l) in self.stages:
            for c in range(NCH):
                slot = self.wslots.get()
                s3 = slot[:, :3968].re("p (j n) -> p j n", n=128)
                for j in range(31):
                    self.ts(s3[:, j, :], self.identf, pv[:, P_CW + c * 31 + j:P_CW + c * 31 + j + 1], None, ALU.mult)
                self.dma(V(self.wbf[l][NBLK + c].buf, self.wbf[l][NBLK + c].ap[:, :3968]), slot[:, :3968], "dg%d" % (c % 2))
                self.wslots.put(slot)
        if ada:
            for j in range(12):
                self.ada_piece(l, j)
            self.ada_finish(l)

    def ada_piece(self, l, j):
        LPl = self.LP[l]
        pv = LPl["pvec"]
        ps = self.PS.get()
        wt = [self.PF.get() for _ in range(8)]
        for kc in range(8):
            self.dma(wt[kc], V(self.wada[l].buf, self.wada[l].ap[j][:, kc * 512:(kc + 1) * 512]), "ada%d" % kc)
        for m in range(4):
            col = m * 2
            for kc in range(8):
                self.mm(ps[:, col:col + 2], wt[kc][:, m * 128:(m + 1) * 128], self.scT[:, kc * 2:kc * 2 + 2], kc == 0, kc == 7)
        self.PF.put(*wt)
        for w in range(2):
            dst = LPl["adaA"][:, w, 4 * j:4 * j + 4] if j < 4 else LPl["adaB"][:, w, 4 * (j - 4):4 * (j - 4) + 4]
            self.tt(dst, ps[:, w:8:2], pv[:, P_BADA + 4 * j:P_BADA + 4 * j + 4], ALU.add)
        self.PS.put(ps)
        if j == 3:
            for w in range(2):
                self.stt(LPl["A1"][:, w, :], LPl["adaA"][:, w, 8:16], 1.0, pv[:, P_N1G:P_N1G + 8], ALU.add, ALU.mult)

    def ada_finish(self, l):
        LPl = self.LP[l]
        pv = LPl["pvec"]
        for w in range(2):
            self.stt(LPl["A2"][:, w, :], LPl["adaB"][:, w, 16:24], 1.0, pv[:, P_N2G:P_N2G + 8], ALU.add, ALU.mult)

    def prefetch_h(self, src, t0, T):
        k = 1 - self.hsel
        q = "pool" if self.pool_dma_ok else "sp"
        for c in range(NCH):
            self.dma(self.hTs[k][c][:, :T], V(src.buf, src.ap[c * 128:(c + 1) * 128, t0:t0 + T]), "h%d" % c, q=q)
        self.h_loaded = (id(src.buf), t0, T)

    def load_h(self, src, t0, T):
        if self.h_loaded != (id(src.buf), t0, T):
            self.prefetch_h(src, t0, T)
        self.hsel = 1 - self.hsel
        self.hT = self.hTs[self.hsel]
        self.h_loaded = None

    def norm_mod(self, T, A, Bv, out_bf=True, outs=None):
        ps = self.PS.get()
        for c in range(NCH):
            sq = self.PB.get()
            self.act(sq[:, :T], self.hT[c][:, :T], AF.Square)
            self.mm(ps[:, :T], self.onesDb, sq[:, :T], c == 0, c == NCH - 1)
            self.PB.put(sq)
        rstd = self.PF.get()
        self.rsqrt_(rstd[:, :T], ps[:, :T])
        self.PS.put(ps)
        res = []
        for c in range(NCH):
            tmp = self.PF.get()
            self.tt(tmp[:, :T], self.hT[c][:, :T], rstd[:, :T], ALU.mult)
            if out_bf:
                o = self.PB.get()
            else:
                o = outs[c]
            if Bv is None:
                self.ts(o[:, :T], tmp[:, :T], A[:, c:c + 1], None, ALU.mult)
            else:
                self.act(o[:, :T], tmp[:, :T], AF.Identity, bias=Bv[:, c:c + 1], scale=A[:, c:c + 1])
            self.PF.put(tmp)
            res.append(o)
        self.PF.put(rstd)
        return res

    def proj_fm(self, ps, slot, hl, m, T, ncols=512, cw=128):
        w3 = slot.re("p (k n) -> p k n", n=ncols)
        for kc in range(NCH):
            self.mm(ps[:cw, :T] if cw < 128 else ps[:, :T], w3[:, kc, m * cw:(m + 1) * cw], hl[kc][:, :T], kc == 0, kc == NCH - 1)

    def adn_proj(self, hl, T):
        w3 = self.adnw.re("p (k n) -> p k n", n=32)
        res = []
        for e in range(2):
            ps = self.PS.get()
            for kc in range(NCH):
                self.mm(ps[:16, :T], w3[:, kc, e * 16:(e + 1) * 16], hl[kc][:, :T], kc == 0, kc == NCH - 1)
            o = self.PB.get()
            self.cp(o[:16, :T], ps[:16, :T], e="act")
            self.PS.put(ps)
            res.append(o)
        return res

    def gla_head_prep(self, h, slot, hl, adn, T, need_q, l=None, idx=None):
        nchk = T // 128
        R = {}
        reuse = l is not None and l in self.gsc
        if reuse and need_q:
            return self.gla_head_prep_p2(h, slot, hl, adn, T, self.gsc[l][idx], self.gdec[l][idx])
        store = reuse and not need_q
        Gq, Gk, dec = [None, None], [None, None], [None, None]
        for e in range(2):
            ps = self.PS.get()
            self.mm(ps[:, :T], self.aupb[:, e * 512 + h * 128:e * 512 + (h + 1) * 128], adn[e][:16, :T], True, True)
            sp = self.PF.get()
            self.act(sp[:, :T], ps[:, :T], AF.Exp, bias=self.negab[:, e * 4 + h:e * 4 + h + 1], scale=-1.0)
            self.PS.put(ps)
            self.act(sp[:, :T], sp[:, :T], AF.Ln, bias=1.0)
            cs = self.PF.get()
            for c in range(nchk):
                sl = slice(c * 128, (c + 1) * 128)
                self.tk.op("dve", lambda E, cs=cs, sl=sl, sp=sp: E.tensor_tensor_scan(cs.ap[:, sl], self.onesf.ap, sp.ap[:, sl], 0.0, ALU.mult, ALU.add), [self.onesf, sp], [cs])
            dec[e] = self.small.get()
            if e == 0:
                self.act(dec[e][:, :nchk], cs[:, 127:T:128], AF.Exp, scale=-1.0 / 16)
                bsrc, sgn = cs, -1.0
            else:
                for c in range(nchk):
                    sl = slice(c * 128, (c + 1) * 128)
                    self.stt(sp[:, sl], cs[:, sl], cs[:, c * 128 + 127:c * 128 + 128], sp[:, sl], ALU.subtract, ALU.subtract)
                self.act(dec[e][:, :nchk], sp[:, 0:T:128], AF.Exp, scale=1.0 / 16)
                bsrc, sgn = sp, 1.0
            if need_q:
                Gq[e] = self.PF.get()
                self.act(Gq[e][:, :T], bsrc[:, :T], AF.Exp, bias=self.LNSC, scale=sgn / 16)
            Gk[e] = self.PF.get()
            self.act(Gk[e][:, :T], bsrc[:, :T], AF.Exp, scale=-sgn / 16)
            self.PF.put(sp, cs)
        if need_q:
            ps = self.PS.get()
            self.proj_fm(ps, slot, hl, 0, T)
            R["qe"] = []
            for e in range(2):
                o = self.PB.get()
                self.tt(o[:, :T], ps[:, :T], Gq[e][:, :T], ALU.mult)
                self.PF.put(Gq[e])
                R["qe"].append(o)
            self.PS.put(ps)
        ps = self.PS.get()
        self.proj_fm(ps, slot, hl, 1, T)
        R["ke"] = []
        kdfm = []
        for e in range(2):
            k32 = self.PF.get()
            self.tt(k32[:, :T], ps[:, :T], Gk[e][:, :T], ALU.mult)
            self.PF.put(Gk[e])
            if need_q or store:
                o = self.PB.get()
                self.cp(o[:, :T], k32[:, :T], e="act")
                R["ke"].append(o)
            kd = self.PB.get()
            for c in range(nchk):
                sl = slice(c * 128, (c + 1) * 128)
                self.act(kd[:, sl], k32[:, sl], AF.Identity, scale=dec[e][:, c:c + 1])
            self.PF.put(k32)
            kdfm.append(kd)
        self.PS.put(ps)
        for e in range(2):
            for c in range(nchk):
                col = (e * 4 + c) * 128
                self.tr(self.pst[:, col:col + 128], kdfm[e][:, c * 128:(c + 1) * 128])
        self.kd_t = [self.PB.get(), self.PB.get()]
        self.vt_t = [self.PB.get(), self.PB.get()]
        for e in range(2):
            self.cp(self.kd_t[e][:, :nchk * 128], self.pst[:, e * 512:e * 512 + nchk * 128], e="act")
        self.PB.put(*kdfm)
        w3 = slot.re("p (k n) -> p k n", n=512)
        for c0 in range(0, nchk, 2):
            ps = self.PS.get()
            for c in range(c0, c0 + 2):
                for kc in range(NCH):
                    self.mm(ps[:, (c - c0) * 256:(c - c0 + 1) * 256], hl[kc][:, c * 128:(c + 1) * 128], w3[:, kc, 256:512], kc == 0, kc == NCH - 1)
            self.cp(self.vt_t[c0 // 2], ps[:, :], e="act")
            self.PS.put(ps)
        R["dec"] = dec
        if store:
            g, gd = self.gsc[l][idx], self.gdec[l][idx]
            for e in range(2):
                self.dma(V(g.buf, g.ap[:, e * 512:e * 512 + T]), R["ke"][e][:, :T], "gs%d" % e)
                self.dma(V(g.buf, g.ap[:, 1024 + e * 512:1024 + e * 512 + nchk * 128]), self.kd_t[e][:, :nchk * 128], "gs%d" % (2 + e))
                self.dma(V(gd.buf, gd.ap[:, e * 4:e * 4 + nchk]), dec[e][:, :nchk], "gs%d" % (5 + e))
            for c0 in range(0, nchk, 2):
                self.dma(V(g.buf, g.ap[:, 2048 + c0 * 256:2048 + (c0 + 2) * 256]), self.vt_t[c0 // 2], "gs%d" % (4 if c0 == 0 else 7))
            self.PB.put(*R["ke"])
            R["ke"] = []
        return R

    def gla_head_prep_p2(self, h, slot, hl, adn, T, g, gd):
        nchk = T // 128
        R = {"qe": [], "ke": [], "dec": []}
        Gq = [None, None]
        for e in range(2):
            ps = self.PS.get()
            self.mm(ps[:, :T], self.aupb[:, e * 512 + h * 128:e * 512 + (h + 1) * 128], adn[e][:16, :T], True, True)
            sp = self.PF.get()
            self.act(sp[:, :T], ps[:, :T], AF.Exp, bias=self.negab[:, e * 4 + h:e * 4 + h + 1], scale=-1.0)
            self.PS.put(ps)
            self.act(sp[:, :T], sp[:, :T], AF.Ln, bias=1.0)
            cs = self.PF.get()
            for c in range(nchk):
                sl = slice(c * 128, (c + 1) * 128)
                self.tk.op("dve", lambda E, cs=cs, sl=sl, sp=sp: E.tensor_tensor_scan(cs.ap[:, sl], self.onesf.ap, sp.ap[:, sl], 0.0, ALU.mult, ALU.add), [self.onesf, sp], [cs])
            if e == 0:
                bsrc, sgn = cs, -1.0
            else:
                for c in range(nchk):
                    sl = slice(c * 128, (c + 1) * 128)
                    self.stt(sp[:, sl], cs[:, sl], cs[:, c * 128 + 127:c * 128 + 128], sp[:, sl], ALU.subtract, ALU.subtract)
                bsrc, sgn = sp, 1.0
            Gq[e] = self.PF.get()
            self.act(Gq[e][:, :T], bsrc[:, :T], AF.Exp, bias=self.LNSC, scale=sgn / 16)
            self.PF.put(sp, cs)
        ps = self.PS.get()
        self.proj_fm(ps, slot, hl, 0, T)
        for e in range(2):
            o = self.PB.get()
            self.tt(o[:, :T], ps[:, :T], Gq[e][:, :T], ALU.mult)
            self.PF.put(Gq[e])
            R["qe"].append(o)
        self.PS.put(ps)
        q = "pool" if self.pool_dma_ok else "sp"
        self.kd_t = [self.PB.get(), self.PB.get()]
        self.vt_t = [self.PB.get(), self.PB.get()]
        for e in range(2):
            o = self.PB.get()
            self.dma(o[:, :T], V(g.buf, g.ap[:, e * 512:e * 512 + T]), "gl%d" % e, q=q)
            R["ke"].append(o)
            self.dma(self.kd_t[e][:, :nchk * 128], V(g.buf, g.ap[:, 1024 + e * 512:1024 + e * 512 + nchk * 128]), "gl%d" % (2 + e), q=q)
            d = self.small.get()
            self.dma(d[:, :nchk], V(gd.buf, gd.ap[:, e * 4:e * 4 + nchk]), "gl%d" % (5 + e), q=q)
            R["dec"].append(d)
        for c0 in range(0, nchk, 2):
            self.dma(self.vt_t[c0 // 2], V(g.buf, g.ap[:, 2048 + c0 * 256:2048 + (c0 + 2) * 256]), "gl%d" % (4 if c0 == 0 else 7), q=q)
        return R

    def kv(self, ps, e, c):
        self.mm(ps, self.kd_t[e][:, c * 128:(c + 1) * 128], self.vt_t[c // 2][:, (c % 2) * 256:(c % 2 + 1) * 256], True, True)

    def free_kdvt(self):
        self.PB.put(*self.kd_t)
        self.PB.put(*self.vt_t)

    def p1_tile(self, l, src, t0, T, which, is_ctx, n, nxt=None):
        nchk = T // 128
        self.load_h(src, t0, T)
        if nxt is not None:
            self.prefetch_h(*nxt)
        hl = self.norm_mod(T, self.A1[:, which, :], self.adaA[:, which, 0:8])
        adn = self.adn_proj(hl, T)
        if not is_ctx:
            for h in range(4):
                self.dma(V(self.sbs[l][n].buf, self.sbs[l][n].ap[:, h * 256:(h + 1) * 256]), self.Sb[h], "sbs%d" % h)
            self.dma(V(self.sbs[l][n].buf, self.sbs[l][n].ap[:, 1024:1028]), self.Abc, "sbs4")
        for h in range(4):
            slot = self.ws_next(l, B_QKV + h)
            R = self.gla_head_prep(h, slot, hl, adn, T, False, l, (self.NT if is_ctx else n) * 4 + h)
            self.ws_free(slot)
            dec = R["dec"]
            for c in range(nchk - 1, -1, -1):
                ps = self.PS.get()
                self.kv(ps[:, :256], 1, c)
                self.stt(self.Sb[h], self.Sb[h], dec[1][:, c:c + 1], ps[:, :256], ALU.mult, ALU.add)
                self.PS.put(ps)
                self.tt(self.Abc[:, h:h + 1], self.Abc[:, h:h + 1], dec[1][:, c:c + 1], ALU.mult, e="pool")
            df = self.small.get()
            for c in range(nchk):
                ps = self.PS.get()
                self.kv(ps[:, :256], 0, c)
                if c == 0:
                    self.cp(self.Tf, ps[:, :256])
                    self.cp(df[:, 0:1], dec[0][:, 0:1], e="pool")
                else:
                    self.stt(self.Tf, self.Tf, dec[0][:, c:c + 1], ps[:, :256], ALU.mult, ALU.add)
                    self.tt(df[:, 0:1], df[:, 0:1], dec[0][:, c:c + 1], ALU.mult, e="pool")
                self.PS.put(ps)
            self.stt(self.Sacc[h], self.Tf, self.Dsuf[:, h:h + 1], self.Sacc[h], ALU.mult, ALU.add)
            if l == self.stages[0][1]:
                self.pace(self.Sacc[h], 1, 2)
            self.tt(self.Dsuf[:, h:h + 1], self.Dsuf[:, h:h + 1], df[:, 0:1], ALU.mult)
            self.small.put(df, dec[0], dec[1])
            self.free_kdvt()
        self.PB.put(*hl)
        self.PB.put(*adn)

    def p1_reset(self):
        P = "dve"
        for h in range(4):
            self.tk.op(P, lambda E, h=h: E.memset(self.Sb[h].ap, 0.0), [], [self.Sb[h]])
            self.tk.op(P, lambda E, h=h: E.memset(self.Sacc[h].ap, 0.0), [], [self.Sacc[h]])
        self.tk.op(P, lambda E: E.memset(self.Dsuf.ap, 1.0), [], [self.Dsuf])
        self.tk.op(P, lambda E: E.memset(self.Abc.ap, 1.0), [], [self.Abc])

    def stage_p1(self, l):
        recs = self.rec_mine[l]

        def recv(c0, c1):
            p = c0 // 1024
            return V(recs[p].buf, recs[p].ap[:, c0 - p * 1024:c1 - p * 1024])
        self.p1_reset()
        self.p1_tile(l, self.csrc[l], 0, CTX, 1, True, None, nxt=(self.hsrc[l], (self.NT - 1) * TL, TL))
        for h in range(4):
            self.dma(recv(2048 + h * 256, 2048 + (h + 1) * 256), self.Sacc[h], "rec%d" % h)
            self.dma(recv(3072 + h * 256, 3072 + (h + 1) * 256), self.Sb[h], "rec%d" % (4 + h))
        self.p1_reset()
        for n in range(self.NT - 1, -1, -1):
            self.p1_tile(l, self.hsrc[l], n * TL, TL, 0, False, n, nxt=((self.hsrc[l], (n - 1) * TL, TL) if n > 0 else None))
        for h in range(4):
            self.dma(recv(h * 256, (h + 1) * 256), self.Sacc[h], "rec%d" % h)
            self.dma(recv(1024 + h * 256, 1024 + (h + 1) * 256), self.Sb[h], "rec%d" % (4 + h))
        zt = self.PF.get()
        self.tk.op("dve", lambda E, zt=zt: E.memset(zt.ap[:, :128], 0.0), [], [zt])
        self.dma(recs[4], zt[:, :128], "rec8")
        self.PF.put(zt)
        self.dma(recv(4096, 4100), self.Dsuf, "rec8")
        self.dma(recv(4100, 4104), self.Abc, "rec9")

    def stage_x(self, l):
        ras = self.rec_all[l]
        if self.fused:
            for p in range(5):
                mine, ra = self.rec_mine[l][p], ras[p]

                def emit(tk, mine=mine, ra=ra):
                    tk._wait("pool", tk._deps([mine], [ra]))
                    ins = self.nc.gpsimd.collective_compute("AllGather", ALU.bypass, replica_groups=[[0, 1, 2, 3], [4, 5, 6, 7]], ins=[mine.ap.opt()], outs=[ra.ap.opt()])
                    sem = tk._sem("cc")
                    tk.cnt["cc"] += 1
                    ins.then_inc(sem, 1)
                    self.nc.gpsimd.wait_ge(sem, tk.cnt["cc"])
                    tk.seen["pool"]["cc"] = tk.cnt["cc"]
                    ra.buf.w = ("cc", tk.cnt["cc"])
                    ra.buf.r = {}
                    mine.buf.r["cc"] = tk.cnt["cc"]

                self.tk.custom("pool", emit, [mine], [ra], 40000.0)

        def rv(i, c0, c1):
            p = c0 // 1024
            return V(ras[p].buf, ras[p].ap[i * 128:(i + 1) * 128, c0 - p * 1024:c1 - p * 1024])

        for (dst, base, ctxo, dco, order, fo) in ((self.Sf, 0, 2048, 4096, range(4), 0), (self.Sbin, 1024, 3072, 4100, range(3, -1, -1), 4)):
            for h in range(4):
                self.dma(dst[h], rv(0, ctxo + h * 256, ctxo + (h + 1) * 256), "xs%d" % h)
            for i in order:
                ai = self.small.get()
                self.dma(ai[:, 0:4], rv(i, dco, dco + 4), "xsa")
                ae = self.small.get()
                self.ts(ae[:, 0:4], ai[:, 0:4], 1.0, self.flags[:, fo + i:fo + i + 1], ALU.subtract, ALU.mult)
                self.ts(ae[:, 0:4], ae[:, 0:4], 1.0, None, ALU.add)
                for h in range(4):
                    sl = self.PF.get()
                    self.dma(sl[:, :256], rv(i, base + h * 256, base + (h + 1) * 256), "xs%d" % h)
                    self.ts(sl[:, :256], sl[:, :256], self.flags[:, fo + i:fo + i + 1], None, ALU.mult)
                    self.stt(dst[h], dst[h], ae[:, h:h + 1], sl[:, :256], ALU.mult, ALU.add)
                    self.PF.put(sl)
                self.small.put(ai, ae)

    def gated_out(self, l, gblk, oblk, xin, first):
        T = self.T
        for half in range(2):
            gs = self.ws_next(l, gblk + half)
            os_ = self.ws_next(l, oblk + half)
            o3 = os_.re("p (k n) -> p k n", n=512)
            for m in range(4):
                mc = half * 4 + m
                psg = self.PS.get()
                self.proj_fm(psg, gs, self.hl, m, T)
                sig = self.PF.get()
                self.act(sig[:, :T], psg[:, :T], AF.Sigmoid)
                self.PS.put(psg)
                ps = self.PS.get()
                for kc in range(NCH):
                    self.mm(ps[:, :T], o3[:, kc, m * 128:(m + 1) * 128], xin[kc][:, :T], kc == 0, kc == NCH - 1)
                if first:
                    self.tt(self.yacc[mc][:, :T], ps[:, :T], sig[:, :T], ALU.mult)
                else:
                    self.tt(sig[:, :T], ps[:, :T], sig[:, :T], ALU.mult)
                    self.tt(self.yacc[mc][:, :T], self.yacc[mc][:, :T], sig[:, :T], ALU.add)
                self.PS.put(ps)
                self.PF.put(sig)
            self.ws_free(gs)
            self.ws_free(os_)

    def p2_tile(self, l, src, dst, t0, T, which, is_ctx, n, final, nxt=None):
        self.T = T
        self.dbg_on = (not is_ctx) and n == 0 and l == 0
        nchk = T // 128
        pv = self.pvec
        self.load_h(src, t0, T)
        hl = self.hl = self.norm_mod(T, self.A1[:, which, :], self.adaA[:, which, 0:8])
        adn = self.adn_proj(hl, T)
        self.dbg("hl", hl, T)
        if is_ctx:
            for h in range(4):
                self.tk.op("dve", lambda E, h=h: E.memset(self.Sf[h].ap, 0.0), [], [self.Sf[h]])
                self.tk.op("dve", lambda E, h=h: E.memset(self.Sb[h].ap, 0.0), [], [self.Sb[h]])
        else:
            ab = self.small.get()
            self.dma(ab[:, 0:4], V(self.sbs[l][n].buf, self.sbs[l][n].ap[:, 1024:1028]), "sbl")
            for h in range(4):
                self.dma(self.Sb[h], V(self.sbs[l][n].buf, self.sbs[l][n].ap[:, h * 256:(h + 1) * 256]), "sbl%d" % h)
                self.stt(self.Sb[h], self.Sbin[h], ab[:, h:h + 1], self.Sb[h], ALU.mult, ALU.add)
            self.small.put(ab)
        on32 = []
        on = []
        for h in range(4):
            slot = self.ws_next(l, B_QKV + h)
            R = self.gla_head_prep(h, slot, hl, adn, T, True, l, (self.NT if is_ctx else n) * 4 + h)
            self.ws_free(slot)
            dec, qe, ke = R["dec"], R["qe"], R["ke"]
            for c in range(nchk):
                self.cp(self.Sbf[0][c], self.Sf[h], e="act")
                ps = self.PS.get()
                self.kv(ps[:, :256], 0, c)
                self.stt(self.Sf[h], self.Sf[h], dec[0][:, c:c + 1], ps[:, :256], ALU.mult, ALU.add)
                self.PS.put(ps)
            for c in range(nchk - 1, -1, -1):
                self.cp(self.Sbf[1][c], self.Sb[h], e="act")
                ps = self.PS.get()
                self.kv(ps[:, :256], 1, c)
                self.stt(self.Sb[h], self.Sb[h], dec[1][:, c:c + 1], ps[:, :256], ALU.mult, ALU.add)
                self.PS.put(ps)
            self.small.put(dec[0], dec[1])
            Am = []
            for e in range(2):
                ps = self.PS.get()
                for c in range(nchk):
                    sl = slice(c * 128, (c + 1) * 128)
                    self.mm(ps[:, sl], ke[e][:, sl], qe[e][:, sl], True, True)
                a = self.PB.get()
                self.tt(a[:, :T], ps[:, :T], (self.maskf if e == 0 else self.maskb)[:, :T], ALU.mult)
                self.PS.put(ps)
                Am.append(a)
            o32 = []
            for vc in range(2):
                ps = self.PS.get()
                for c in range(nchk):
                    sl = slice(c * 128, (c + 1) * 128)
                    vl = self.vt_t[c // 2][:, (c % 2) * 256 + vc * 128:(c % 2) * 256 + (vc + 1) * 128]
                    self.mm(ps[:, sl], vl, Am[0][:, sl], True, False)
                    self.mm(ps[:, sl], vl, Am[1][:, sl], False, False)
                    self.mm(ps[:, sl], self.Sbf[0][c][:, vc * 128:(vc + 1) * 128], qe[0][:, sl], False, False)
                    self.mm(ps[:, sl], self.Sbf[1][c][:, vc * 128:(vc + 1) * 128], qe[1][:, sl], False, True)
                o = self.PF.get()
                self.cp(o[:, :T], ps[:, :T], e="act")
                self.PS.put(ps)
                o32.append(o)
            self.PB.put(*Am)
            self.PB.put(*qe)
            self.PB.put(*ke)
            self.free_kdvt()
            ps = self.PS.get()
            for vc in range(2):
                sq = self.PB.get()
                self.act(sq[:, :T], o32[vc][:, :T], AF.Square)
                self.mm(ps[:, :T], self.ones256b, sq[:, :T], vc == 0, vc == 1)
                self.PB.put(sq)
            rstd = self.PF.get()
            self.rsqrt_(rstd[:, :T], ps[:, :T])
            self.PS.put(ps)
            for vc in range(2):
                hc = 2 * h + vc
                self.stt(o32[vc][:, :T], o32[vc][:, :T], pv[:, P_GNG + hc:P_GNG + hc + 1], rstd[:, :T], ALU.mult, ALU.mult)
                on32.append(o32[vc])
            self.PF.put(rstd)
            if h % 2 == 1:
                half = h // 2
                slot = self.ws_next(l, B_R + half)
                for m in range(4):
                    hc = half * 4 + m
                    ps = self.PS.get()
                    self.proj_fm(ps, slot, hl, m, T)
                    sr = self.PF.get()
                    self.act(sr[:, :T], ps[:, :T], AF.Silu)
                    self.PS.put(ps)
                    o = self.PB.get()
                    self.tt(o[:, :T], on32[hc][:, :T], sr[:, :T], ALU.mult)
                    self.PF.put(sr, on32[hc])
                    on.append(o)
                self.ws_free(slot)
        self.PB.put(*adn)
        self.dbg("on", on, T)
        self.gated_out(l, B_GA, B_OGLA, on, True)
        self.dbg("ya", self.yacc, T)
        self.PB.put(*on)
        W = 256 if is_ctx else 64
        WP = W + 30
        NR = T // W
        mode = "ctx" if is_ctx else "lat"
        if self.pad_mode != mode:
            for hb in self.hcp:
                self.tk.op("dve", lambda E, hb=hb: E.memset(hb.ap, 0.0), [], [hb])
            self.pad_mode = mode
        acc = []
        accb = []
        for i in range(4):
            slot = self.ws_next(l, B_CV + i)
            for j in range(2):
                c = 2 * i + j
                ps1 = self.PS.get()
                self.proj_fm(ps1, slot, hl, j, T)
                ps2 = self.PS.get()
                self.proj_fm(ps2, slot, hl, 2 + j, T)
                sg = self.PF.get()
                self.act(sg[:, :T], ps2[:, :T], AF.Sigmoid)
                self.PS.put(ps2)
                hb = self.hcp[self.hcp_i]
                self.hcp_i ^= 1
                h3 = hb[:, :NR * WP].re("p (r w) -> p r w", w=WP)
                self.tt(h3[:, :, 15:15 + W], ps1[:, :T].re("p (r w) -> p r w", w=W), sg[:, :T].re("p (r w) -> p r w", w=W), ALU.mult)
                self.PS.put(ps1)
                self.PF.put(sg)
                dslot = self.ws_next(l, NBLK + c)
                d3 = dslot[:, :3968].re("p (j n) -> p j n", n=128)
                ps = self.PS.get()
                p3 = ps[:, :T].re("p (r w) -> p r w", w=W)
                for jj in range(31):
                    self.mm(p3, d3[:, jj, :], h3[:, :, jj:jj + W], jj == 0, jj == 30)
                self.ws_free(dslot)
                a = self.PF.get()
                self.act(a[:, :T], ps[:, :T], AF.Identity, bias=pv[:, P_CVB + c:P_CVB + c + 1])
                ab = self.PB.get()
                self.act(ab[:, :T], ps[:, :T], AF.Identity, bias=pv[:, P_CVB + c:P_CVB + c + 1])
                self.PS.put(ps)
                acc.append(a)
                accb.append(ab)
            self.ws_free(slot)
            self.pace(acc[-1], 4)
        psm = self.PS.get()
        psq = self.PS.get()
        for c in range(NCH):
            self.mm(psm[:, :T], self.onesDb, accb[c][:, :T], c == 0, c == NCH - 1)
        self.PB.put(*accb)
        for c in range(NCH):
            sq = self.PB.get()
            self.act(sq[:, :T], acc[c][:, :T], AF.Square)
            self.mm(psq[:, :T], self.onesDb, sq[:, :T], c == 0, c == NCH - 1)
            self.PB.put(sq)
        mean = self.PF.get()
        self.cp(mean[:, :T], psm[:, :T], e="act")
        self.PS.put(psm)
        var = self.PF.get()
        self.tt(var[:, :T], mean[:, :T], mean[:, :T], ALU.mult)
        self.tt(var[:, :T], psq[:, :T], var[:, :T], ALU.subtract)
        self.PS.put(psq)
        self.ts(var[:, :T], var[:, :T], 0.0, None, ALU.max)
        self.rsqrt_(var[:, :T], var[:, :T])
        cvb = []
        for c in range(NCH):
            self.tt(acc[c][:, :T], acc[c][:, :T], mean[:, :T], ALU.subtract)
            self.tt(acc[c][:, :T], acc[c][:, :T], var[:, :T], ALU.mult)
            o = self.PB.get()
            self.act(o[:, :T], acc[c][:, :T], AF.Silu, bias=pv[:, P_CLB + c:P_CLB + c + 1], scale=pv[:, P_CLG + c:P_CLG + c + 1])
            self.PF.put(acc[c])
            cvb.append(o)
        self.PF.put(mean, var)
        self.dbg("cvb", cvb, T)
        self.gated_out(l, B_GB, B_OCONV, cvb, False)
        self.dbg("yb", self.yacc, T)
        self.PB.put(*cvb)
        svg = [[None, None] for _ in range(nchk)]
        s1 = [self.small.get() for _ in range(nchk)]
        for half in range(2):
            slot = self.ws_next(l, B_SV + half)
            w3 = slot.re("p (k n) -> p k n", n=512)
            for tb in range(nchk):
                ps = self.PS.get()
                for kc in range(NCH):
                    self.mm(ps[:, :], hl[kc][:, tb * 128:(tb + 1) * 128], w3[:, kc, :], kc == 0, kc == NCH - 1)
                g = self.PF.get()
                self.act(g, ps, AF.Gelu_apprx_tanh)
                self.PS.put(ps)
                self.tk.op("dve", lambda E, o=s1[tb], half=half, g=g: E.reduce_sum(o.ap[:, half:half + 1], g.ap, AX.X), [g], [s1[tb]])
                sq = self.PF.get()
                self.act(sq, g, AF.Square)
                self.tk.op("dve", lambda E, o=s1[tb], half=half, sq=sq: E.reduce_sum(o.ap[:, 2 + half:3 + half], sq.ap, AX.X), [sq], [s1[tb]])
                self.PF.put(sq)
                svg[tb][half] = g
            self.ws_free(slot)
        svn = []
        for tb in range(nchk):
            s = s1[tb]
            self.tt(s[:, 4:5], s[:, 0:1], s[:, 1:2], ALU.add)
            self.tt(s[:, 5:6], s[:, 2:3], s[:, 3:4], ALU.add)
            self.ts(s[:, 4:6], s[:, 4:6], 1.0 / D, None, ALU.mult)
            self.tt(s[:, 6:7], s[:, 4:5], s[:, 4:5], ALU.mult)
            self.tt(s[:, 6:7], s[:, 5:6], s[:, 6:7], ALU.subtract)
            self.ts(s[:, 6:7], s[:, 6:7], 0.0, None, ALU.max)
            self.rsqrt_(s[:, 7:8], s[:, 6:7])
            self.stt(s[:, 8:9], s[:, 4:5], -1.0, s[:, 7:8], ALU.mult, ALU.mult)
            o2 = [self.PB.get(), self.PB.get()]
            for half in range(2):
                g = svg[tb][half]
                self.act(g, g, AF.Identity, bias=s[:, 8:9], scale=s[:, 7:8])
                self.tt(g, g, self.pbc[:, 1024 + half * 512:1024 + (half + 1) * 512], ALU.mult)
                self.tt(o2[half], g, self.pbc[:, 2048 + half * 512:2048 + (half + 1) * 512], ALU.add)
                self.PF.put(g)
            svn.append(o2)
            self.small.put(s)
        spo = []
        for half in range(2):
            slot = self.ws_next(l, B_SU + half)
            for m in range(4):
                g = half * 4 + m
                ps = self.PS.get()
                self.proj_fm(ps, slot, hl, m, T)
                su = self.PF.get()
                self.act(su[:, :T], ps[:, :T], AF.Gelu_apprx_tanh)
                self.PS.put(ps)
                ps = self.PS.get()
                for tb in range(nchk):
                    self.mm(ps[:, tb * 128:(tb + 1) * 128], svn[tb][half][:, m * 128:(m + 1) * 128], self.wsTb[:, g * 128:(g + 1) * 128], True, True)
                tmp = self.PF.get()
                for tb in range(nchk):
                    sl = slice(tb * 128, (tb + 1) * 128)
                    self.tt(tmp[:, sl], ps[:, sl], self.pbc[:, g * 128:(g + 1) * 128], ALU.add)
                self.PS.put(ps)
                o = self.PB.get()
                self.tt(o[:, :T], tmp[:, :T], su[:, :T], ALU.mult)
                self.PF.put(tmp, su)
                spo.append(o)
            self.ws_free(slot)
        for tb in range(nchk):
            self.PB.put(*svn[tb])
        self.dbg("spo", spo, T)
        self.gated_out(l, B_GC, B_OSGU, spo, False)
        self.dbg("yc", self.yacc, T)
        self.PB.put(*spo)
        self.PB.put(*hl)
        yb = []
        for c in range(NCH):
            o = self.PB.get()
            self.cp(o[:, :T], self.yacc[c][:, :T], e="act")
            yb.append(o)
        for half in range(2):
            slot = self.ws_next(l, B_OUT + half)
            o3 = slot.re("p (k n) -> p k n", n=512)
            for m in range(4):
                mc = half * 4 + m
                ps = self.PS.get()
                for kc in range(NCH):
                    self.mm(ps[:, :T], o3[:, kc, m * 128:(m + 1) * 128], yb[kc][:, :T], kc == 0, kc == NCH - 1)
                self.stt(self.hT[mc][:, :T], ps[:, :T], self.adaB[:, which, mc:mc + 1], self.hT[mc][:, :T], ALU.mult, ALU.add)
                self.PS.put(ps)
            self.ws_free(slot)
        self.PB.put(*yb)
        self.dbg("hmid", self.hT, T)
        if nxt is not None:
            self.prefetch_h(*nxt)
        hl2 = self.norm_mod(T, self.A2[:, which, :], self.adaB[:, which, 8:16])
        actv = []
        for j in range(11):
            slot = self.ws_next(l, B_FIN + j)
            for i in range(2):
                psg = self.PS.get()
                self.proj_fm(psg, slot, hl2, i, T)
                psu = self.PS.get()
                self.proj_fm(psu, slot, hl2, 2 + i, T)
                sg = self.PF.get()
                self.act(sg[:, :T], psg[:, :T], AF.Silu)
                self.PS.put(psg)
                o = self.PB.get()
                self.tt(o[:, :T], sg[:, :T], psu[:, :T], ALU.mult)
                self.PS.put(psu)
                self.PF.put(sg)
                actv.append(o)
            self.pace(actv[-1], 4)
            self.ws_free(slot)
        self.PB.put(*hl2)
        for mc in range(NCH):
            slot = self.ws_next(l, B_FOUT + mc)
            o3 = slot[:, :2816].re("p (k n) -> p k n", n=128)
            ps = self.PS.get()
            for kc in range(22):
                self.mm(ps[:, :T], o3[:, kc, :], actv[kc][:, :T], kc == 0, kc == 21)
            self.stt(self.hT[mc][:, :T], ps[:, :T], self.adaB[:, which, 24 + mc:25 + mc], self.hT[mc][:, :T], ALU.mult, ALU.add)
            self.PS.put(ps)
            self.ws_free(slot)
            self.pace(self.hT[mc], 4)
        self.PB.put(*actv)
        if final:
            outs = [self.PF.get() for _ in range(NCH)]
            self.norm_mod(T, pv[:, P_FG:P_FG + 8], None, out_bf=False, outs=outs)
            for c in range(NCH):
                self.dma(V(dst.buf, dst.ap[c * 128:(c + 1) * 128, t0:t0 + T]), outs[c][:, :T], "st%d" % c, q=("pool" if self.pool_dma_ok else "sp"))
            self.PF.put(*outs)
        else:
            for c in range(NCH):
                self.dma(V(dst.buf, dst.ap[c * 128:(c + 1) * 128, t0:t0 + T]), self.hT[c][:, :T], "st%d" % c, q=("pool" if self.pool_dma_ok else "sp"))

    def dbg(self, name, tiles, T):
        if not (DEBUG and self.dbg_on):
            return
        d = self.dout("dbg_" + name, [len(tiles) * 128, T])
        for c, t in enumerate(tiles):
            self.dma(V(d.buf, d.ap[c * 128:(c + 1) * 128, :]), t[:, :T], "dbg", q="pool")

    def dbg2(self, name, v, rows, ncols):
        if not DEBUG:
            return
        d = self.dout("dbg_" + name, [rows, ncols])
        self.dma(d, v, "dbg", q="pool")

    def stage_p2(self, l):
        self.flush_conv(l)
        self.setup_late(l)
        if l == 0:
            self.p2_tile(l, self.csrc[0], self.cdst[0], 0, CTX, 1, True, None, False, nxt=(self.hsrc[l], 0, TL))
        self.pool_dma_ok = (l != self.stages[0][1])
        self.stage_x(l)
        items = self.late_plan(l)
        nsl = max(1, self.NT - 1)
        per = (len(items) + nsl - 1) // nsl
        for n in range(self.NT):
            self.p2_tile(l, self.hsrc[l], self.hdst[l], n * TL, TL, 0, False, n, l == 1, nxt=((self.hsrc[l], (n + 1) * TL, TL) if n + 1 < self.NT else None))
            for it in items[:per]:
                it()
            items = items[per:]
        for it in items:
            it()


def _blk(W, cols):
    K = W.shape[0]
    sub = W[:, cols]
    kc = K // 128
    a = sub.reshape(kc, 128, sub.shape[1]).transpose(1, 0, 2).reshape(128, -1)
    out = np.zeros((128, 4096), np.float32)
    out[:, :a.shape[1]] = a
    return out


def _fm(v):
    return np.ascontiguousarray(v.reshape(-1, 128).T)


def prep_layer(inp, l):
    w_in = inp["w_in"][l]
    r = np.arange
    blks = []
    for h in range(4):
        cols = np.concatenate([Q0 + h * 128 + r(128), K0 + h * 128 + r(128), V0 + h * 256 + r(256)])
        blks.append(_blk(w_in, cols))
    for i in range(2):
        blks.append(_blk(w_in, R0 + i * 512 + r(512)))
    for i in range(4):
        cols = np.concatenate([C10 + i * 256 + r(256), C20 + i * 256 + r(256)])
        blks.append(_blk(w_in, cols))
    for i in range(2):
        blks.append(_blk(w_in, SU0 + i * 512 + r(512)))
    for i in range(2):
        blks.append(_blk(w_in, SV0 + i * 512 + r(512)))
    for i in range(6):
        blks.append(_blk(w_in, GT0 + i * 512 + r(512)))
    for name in ("w_o_gla", "w_o_conv", "w_o_sgu", "w_out"):
        for i in range(2):
            blks.append(_blk(inp[name][l], i * 512 + r(512)))
    wf = inp["w_ffn_in"][l]
    for j in range(11):
        cols = np.concatenate([j * 256 + r(256), DFF + j * 256 + r(256)])
        blks.append(_blk(wf, cols))
    wo = inp["w_ffn_out"][l]
    for mc in range(8):
        blks.append(_blk(wo, mc * 128 + r(128)))
    assert len(blks) == NBLK
    d = {}
    d["wblk%d" % l] = np.stack(blks)
    d["wadn%d" % l] = np.ascontiguousarray(_blk(w_in, A0 + r(32))[:, :256])
    d["wada%d" % l] = np.stack([_blk(inp["w_ada"][l], j * 512 + r(512)) for j in range(12)])
    pv = np.zeros((128, NPV), np.float32)
    pv[:, P_N1G:P_N1G + 8] = _fm(inp["norm1_g"][l])
    pv[:, P_N2G:P_N2G + 8] = _fm(inp["norm2_g"][l])
    pv[:, P_GNG:P_GNG + 8] = _fm(inp["gla_norm_g"][l])
    pv[:, P_CVB:P_CVB + 8] = _fm(inp["conv_b"][l])
    pv[:, P_CLG:P_CLG + 8] = _fm(inp["conv_ln_g"][l])
    pv[:, P_CLB:P_CLB + 8] = _fm(inp["conv_ln_b"][l])
    pv[:, P_BADA:P_BADA + 48] = _fm(inp["b_ada"][l])
    cw = inp["conv_w"][l]
    pv[:, P_CW:P_CW + 248] = cw.T.reshape(8, 128, 31).transpose(1, 0, 2).reshape(128, 248)
    pv[:, P_AB:P_AB + 8] = _fm(inp["gla_a_b"][l].reshape(-1))
    pv[:, P_FG:P_FG + 8] = _fm(inp["final_g"])
    d["pvec%d" % l] = pv
    d["aup%d" % l] = np.ascontiguousarray(inp["gla_a_up"][l].transpose(1, 0, 2).reshape(16, 1024))
    d["wsT%d" % l] = np.ascontiguousarray(inp["sgu_ws"][l].transpose(2, 0, 1).reshape(128, 1024))
    pbc = np.concatenate([inp["sgu_b"][l].reshape(-1), inp["sgu_ln_g"][l], inp["sgu_ln_b"][l]])
    d["pbc%d" % l] = np.ascontiguousarray(np.broadcast_to(pbc[None, :], (128, 3072)))
    return d


_CACHE = {}


def _get(NT, stages, fused):
    key = (NT, tuple(stages), fused)
    if key not in _CACHE:
        _CACHE[key] = Builder(NT, list(stages), fused)
    return _CACHE[key]


def _core_common(inp, core):
    b, j = core // 4, core % 4
    cv = np.zeros((128, 8, 2), np.float32)
    cv[:, :, 0] = _fm(inp["c"][b])
    cv[:, :, 1] = _fm(inp["c_ctx"])
    fl = np.zeros((128, 8), np.float32)
    for i in range(4):
        fl[:, i] = 1.0 if i < j else 0.0
        fl[:, 4 + i] = 1.0 if i > j else 0.0
    return {"cvec": cv.reshape(128, 16), "flags": fl}


FUSED = True
REUSE = True
SCHED_WINDOW = 3000
SCHED_QUANT = 300.0
SCHED_XLAT = 700.0
SCHED_SELF_LAT = 100.0
DEBUG = False


def kernel(**inp):
    inp = {k: np.asarray(v, np.float32) for k, v in inp.items()}
    x = inp["x"]
    B, S, _ = x.shape
    slab = S // 4
    NT = slab // TL
    lay = [prep_layer(inp, l) for l in range(2)]
    xT = [np.ascontiguousarray(x[c // 4, (c % 4) * slab:(c % 4 + 1) * slab, :].T) for c in range(8)]
    cT = [np.ascontiguousarray(inp["ctx"][b].T) for b in range(2)]
    cores = list(range(8))
    com = [_core_common(inp, c) for c in cores]
    if FUSED:
        bd = _get(NT, [("P1", 0), ("P2", 0), ("P1", 1), ("P2", 1)], True)
        maps = []
        for c in cores:
            m = dict(com[c])
            m.update(lay[0])
            m.update(lay[1])
            m["xT"] = xT[c]
            m["ctxT"] = cT[c // 4]
            maps.append(m)
        res = run_bass_kernel_spmd(bd.nc, maps, core_ids=cores).results
        outT = [res[c]["outT"] for c in cores]
    else:
        bdA = _get(NT, [("P1", 0)], False)
        maps = []
        for c in cores:
            m = dict(com[c]); m.update(lay[0]); m["xT"] = xT[c]; m["ctxT"] = cT[c // 4]
            maps.append(m)
        rA = run_bass_kernel_spmd(bdA.nc, maps, core_ids=cores).results
        bdB = _get(NT, [("P2", 0), ("P1", 1)], False)
        maps = []
        for c in cores:
            m = dict(com[c]); m.update(lay[0]); m.update(lay[1]); m["xT"] = xT[c]; m["ctxT"] = cT[c // 4]
            b = c // 4
            m["st_in"] = np.concatenate([rA[b * 4 + i]["st_out"] for i in range(4)], axis=0)
            m["sbs_in"] = rA[c]["sbs_out"]
            maps.append(m)
        rB = run_bass_kernel_spmd(bdB.nc, maps, core_ids=cores).results
        bdC = _get(NT, [("P2", 1)], False)
        maps = []
        for c in cores:
            m = dict(com[c]); m.update(lay[1]); m["h1"] = rB[c]["h1"]
            b = c // 4
            m["st_in"] = np.concatenate([rB[b * 4 + i]["st_out"] for i in range(4)], axis=0)
            m["sbs_in"] = rB[c]["sbs_out"]
            maps.append(m)
        rC = run_bass_kernel_spmd(bdC.nc, maps, core_ids=cores).results
        outT = [rC[c]["outT"] for c in cores]
    out = np.empty((B, S, D), np.float32)
    for c in cores:
        out[c // 4, (c % 4) * slab:(c % 4 + 1) * slab, :] = outT[c].T
    return out
```
